# Optimizing a Trainium2 kernel written in Bass

```python
import jax, jax.numpy as jnp
from jax import lax
import numpy as np

D_MODEL = 1024
BATCH = 2
SEQ = 8192
DEPTH = 1

N_META = 16
CHUNK = 128
PAD = CHUNK - N_META
RET_HEADS = 4
RET_DK = D_MODEL // RET_HEADS
RET_DV = 2 * RET_DK
SB_HEADS = D_MODEL // 64
SB_DH = 64
D_FF = -(-8 * D_MODEL // (3 * 256)) * 256
ROPE_BASE = 10000.0
NORM_EPS = 1e-6
GN_EPS = 1e-5
PROJ_SPLIT = (
    RET_HEADS * RET_DK,
    RET_HEADS * RET_DK,
    RET_HEADS * RET_DV,
    RET_HEADS * RET_DV,
    SB_HEADS * SB_DH,
    SB_HEADS * SB_DH,
    SB_HEADS * SB_DH,
    D_MODEL,
    D_MODEL,
)
PROJ_WIDTH = sum(PROJ_SPLIT)

kernel_name = "hybrid_retention_stickbreaking_block"


def rmsnorm(x, g):
    xf = x.astype(jnp.float32)
    y = xf * lax.rsqrt(jnp.mean(xf * xf, axis=-1, keepdims=True) + NORM_EPS)
    return (y * g.astype(jnp.float32)).astype(x.dtype)


def rotary(x, pos):
    half = x.shape[-1] // 2
    inv = ROPE_BASE ** (-jnp.arange(half, dtype=jnp.float32) / half)
    ang = pos[:, None] * inv[None, :]
    cos, sin = jnp.cos(ang), jnp.sin(ang)
    x1, x2 = x[..., :half], x[..., half:]
    return jnp.concatenate([x1 * cos - x2 * sin, x1 * sin + x2 * cos], axis=-1)


def retention_chunkwise(q, k, v):
    B, H, Lp, dk = q.shape
    dv = v.shape[-1]
    n = Lp // CHUNK
    log_g = jnp.log1p(-(2.0 ** (-5.0 - jnp.arange(H, dtype=jnp.float32))))
    idx = jnp.arange(CHUNK, dtype=jnp.float32)
    diff = idx[:, None] - idx[None, :]
    decay = jnp.where(diff >= 0, jnp.exp(log_g[:, None, None] * jnp.maximum(diff, 0.0)), 0.0)
    zeta = jnp.exp(log_g[:, None] * (CHUNK - 1.0 - idx))[:, :, None]
    xi = jnp.exp(log_g[:, None] * (idx + 1.0))[:, :, None]
    g_chunk = jnp.exp(log_g * CHUNK)[:, None, None]

    qc = q.reshape(B, H, n, CHUNK, dk)
    kc = k.reshape(B, H, n, CHUNK, dk)
    vc = v.reshape(B, H, n, CHUNK, dv)
    scores = jnp.einsum('bhncd,bhnsd->bhncs', qc, kc) * decay[:, None]
    inner = jnp.einsum('bhncs,bhnse->bhnce', scores, vc)

    def step(state, inp):
        q_i, k_i, v_i = inp
        cross = jnp.einsum('bhcd,bhde->bhce', q_i, state) * xi
        state = g_chunk * state + jnp.einsum('bhsd,bhse->bhde', k_i, v_i * zeta)
        return state, cross

    s0 = jnp.zeros((B, H, dk, dv), jnp.float32)
    _, cross = lax.scan(step, s0, (qc.transpose(2, 0, 1, 3, 4), kc.transpose(2, 0, 1, 3, 4), vc.transpose(2, 0, 1, 3, 4)))
    out = inner + cross.transpose(1, 2, 0, 3, 4)
    return out.reshape(B, H, Lp, dv)


def head_groupnorm(y):
    mu = jnp.mean(y, axis=-1, keepdims=True)
    var = jnp.mean(jnp.square(y - mu), axis=-1, keepdims=True)
    return (y - mu) * lax.rsqrt(var + GN_EPS)


def stick_breaking(q, k, v):
    Lp, d = q.shape[2], q.shape[3]
    scale = d ** -0.5
    outs = []
    for i in range(Lp // CHUNK):
        lk = (i + 1) * CHUNK
        q_blk = q[:, :, i * CHUNK:lk]
        k_pre, v_pre = k[:, :, :lk], v[:, :, :lk]
        z = jnp.einsum('bhtd,bhsd->bhts', q_blk, k_pre) * scale
        t_pos = i * CHUNK + jnp.arange(CHUNK)
        s_pos = jnp.arange(lk)
        valid = (s_pos[None, :] < t_pos[:, None]) & (s_pos[None, :] >= PAD)
        log_not = jnp.where(valid, jax.nn.log_sigmoid(-z), 0.0)
        after = lax.cumsum(log_not, axis=3, reverse=True) - log_not
        a = jnp.where(valid, jnp.exp(jax.nn.log_sigmoid(z) + after), 0.0)
        outs.append(jnp.einsum('bhts,bhse->bhte', a, v_pre))
    return jnp.concatenate(outs, axis=2)


def hybrid_mixer(hn, w_in, w_ret_out, w_sb_out, w_out):
    B, L, _ = hn.shape
    dtype = hn.dtype
    proj = hn @ w_in
    rq, rk, rv, rg, sq, sk, sv, ga, gb = jnp.split(proj, list(np.cumsum(PROJ_SPLIT)[:-1]), axis=-1)

    def heads(t, h):
        t = t.reshape(B, L, h, -1).transpose(0, 2, 1, 3).astype(jnp.float32)
        return jnp.pad(t, ((0, 0), (0, 0), (PAD, 0), (0, 0)))

    def merge_heads(t):
        t = t[:, :, PAD:]
        return t.transpose(0, 2, 1, 3).reshape(B, L, -1)

    pos = jnp.arange(L + PAD, dtype=jnp.float32) - PAD
    q_r = rotary(heads(rq, RET_HEADS), pos) * (RET_DK ** -0.5)
    k_r = rotary(heads(rk, RET_HEADS), pos)
    y_ret = head_groupnorm(retention_chunkwise(q_r, k_r, heads(rv, RET_HEADS)))
    y_ret = (jax.nn.silu(rg.astype(jnp.float32)) * merge_heads(y_ret)).astype(dtype) @ w_ret_out
    y_sb = merge_heads(stick_breaking(heads(sq, SB_HEADS), heads(sk, SB_HEADS), heads(sv, SB_HEADS)))
    y_sb = y_sb.astype(dtype) @ w_sb_out
    merged = jax.nn.sigmoid(ga) * y_ret + jax.nn.sigmoid(gb) * y_sb
    return merged @ w_out


def swiglu(hn, w_ffn_in, w_ffn_out):
    a, b = jnp.split(hn @ w_ffn_in, 2, axis=-1)
    return (jax.nn.silu(a) * b) @ w_ffn_out


def setup_inputs(seed: int = 0) -> dict:
    key = jax.random.key(seed)
    ks = jax.random.split(key, 12)
    f32 = jnp.float32
    nrm = lambda k, shape, fan: jax.random.normal(k, shape, f32) * (fan ** -0.5)
    gain = lambda k: 1.0 + 0.02 * jax.random.normal(k, (DEPTH, D_MODEL), f32)
    return {
        "x": jax.random.normal(ks[0], (BATCH, SEQ, D_MODEL), f32),
        "meta_tokens": jax.random.normal(ks[1], (N_META, D_MODEL), f32),
        "w_in": nrm(ks[2], (DEPTH, D_MODEL, PROJ_WIDTH), D_MODEL),
        "w_ret_out": nrm(ks[3], (DEPTH, RET_HEADS * RET_DV, D_MODEL), RET_HEADS * RET_DV),
        "w_sb_out": nrm(ks[4], (DEPTH, SB_HEADS * SB_DH, D_MODEL), SB_HEADS * SB_DH),
        "w_out": nrm(ks[5], (DEPTH, D_MODEL, D_MODEL), D_MODEL),
        "w_ffn_in": nrm(ks[6], (DEPTH, D_MODEL, 2 * D_FF), D_MODEL),
        "w_ffn_out": nrm(ks[7], (DEPTH, D_FF, D_MODEL), D_FF),
        "norm_mix_pre": gain(ks[8]),
        "norm_mix_post": gain(ks[9]),
        "norm_ffn_pre": gain(ks[10]),
        "norm_ffn_post": gain(ks[11]),
    }


def reference(x, meta_tokens, w_in, w_ret_out, w_sb_out, w_out, w_ffn_in, w_ffn_out,
              norm_mix_pre, norm_mix_post, norm_ffn_pre, norm_ffn_post):
    B = x.shape[0]
    meta = jnp.broadcast_to(meta_tokens.astype(x.dtype)[None], (B, N_META, x.shape[-1]))
    h = jnp.concatenate([meta, x], axis=1)
    for l in range(DEPTH):
        mix = hybrid_mixer(rmsnorm(h, norm_mix_pre[l]), w_in[l], w_ret_out[l], w_sb_out[l], w_out[l])
        h = h + rmsnorm(mix, norm_mix_post[l])
        ff = swiglu(rmsnorm(h, norm_ffn_pre[l]), w_ffn_in[l], w_ffn_out[l])
        h = h + rmsnorm(ff, norm_ffn_post[l])
    return h[:, N_META:]
```

```python
from contextlib import ExitStack
import numpy as np
import ml_dtypes
import concourse.bass as bass
import concourse.mybir as mybir
from concourse.bass_utils import run_bass_kernel_spmd

F32 = mybir.dt.float32
BF16 = mybir.dt.bfloat16
AF = mybir.ActivationFunctionType
ALU = mybir.AluOpType

D = 1024
NMETA = 16
PAD = 112
DFF = 2816
NFT = DFF // 128
PE, ACT, DVE, POOL, SP = "pe", "act", "dve", "pool", "sp"
ENGS = [PE, ACT, DVE, POOL, SP]
SBUF_LO = 16512
SBUF_HI = 229376
import os
NO_CC = bool(int(os.environ.get('NO_CC', '0')))


class Buf:
    __slots__ = ("name", "w", "r", "sem", "cnt", "excl", "inc")

    def __init__(self, name, excl=False):
        self.name = name
        self.excl = excl
        self.w = None
        self.r = {}
        self.sem = None
        self.cnt = 0


class Sched:
    def __init__(self):
        self.ops = {e: [] for e in ENGS}
        self.cnt = {e: 0 for e in ENGS}
        self.seen = {e: {} for e in ENGS}
        self.dsems = []
        self.owners = []

    def _waits(self, eng, reads, writes):
        ev = []
        for b in reads:
            if b.w is not None:
                ev.append(b.w)
        for b in writes:
            if b.w is not None:
                ev.append(b.w)
            ev.extend(b.r.items())
        out = {}
        for k, v in ev:
            if k == eng and eng == PE:
                continue
            if self.seen[eng].get(k, 0) >= v:
                continue
            out[k] = max(out.get(k, 0), v)
        for k, v in out.items():
            self.seen[eng][k] = v
        return list(out.items())

    def op(self, eng, fn, reads=(), writes=(), signal=True):
        writes = list(writes) + [b for b in reads if b.excl and b not in writes]
        waits = self._waits(eng, reads, writes)
        if signal:
            self.cnt[eng] += 1
            v = self.cnt[eng]
        else:
            v = self.cnt[eng] + 1
        for b in reads:
            b.r[eng] = max(b.r.get(eng, 0), v)
        for b in writes:
            b.w = (eng, v)
            b.r = {}
        self.ops[eng].append((waits, fn, (eng, 1) if signal else None))

    def dma(self, q, fn, reads, writes, owner, inc=16):
        waits = self._waits(q, reads, writes)
        if owner.sem is None:
            owner.sem = "d%d" % len(self.dsems)
            self.dsems.append(owner.sem)
            self.owners.append(owner)
        owner.cnt += 1
        ev = (owner.sem, inc * owner.cnt)
        for b in reads:
            b.r[ev[0]] = max(b.r.get(ev[0], 0), ev[1])
        for b in writes:
            b.w = ev
            b.r = {}
        self.ops[q].append((waits, fn, (owner.sem, inc)))
        owner.inc = inc

    def final_waits(self, eng, bufs):
        waits = self._waits(eng, bufs, ())
        self.ops[eng].append((waits, None, None))


def build(nsc):
    phase = 1
    nch = 1 + 4 * nsc
    LP = 128 * nch
    NT = nsc
    nc = bass.Bass("TRN2", target_bir_lowering=False)
    S = Sched()

    def din(name, shape, dt=F32):
        return nc.dram_tensor(name, list(shape), dt, kind="ExternalInput").ap()

    xall = din("xall", [LP, D])
    xown = din("xown", [NT * 128, D])
    w1 = din("w1", [D, 2304])
    wg = din("wg", [D, 2048])
    wro = din("wro", [512, D])
    wso = din("wso", [256, D])
    wout = din("wout", [D, D])
    wfi = din("wfi", [D, 2 * DFF])
    wfo = din("wfo", [DFF, D])
    gam = din("gam", [128, 16])
    gpost = din("gpost", [128, 2 * D])
    rtab = din("rtab", [4, 128, LP])
    gch = din("gch", [128, 1])
    cbf = din("cbf", [128, 128 * 4 + 512 * 6], BF16)
    rsin = [nc.dram_tensor("rsin%d" % k, [512, 2048], F32).ap() for k in range(nsc + 1)]
    rsout = [nc.dram_tensor("rsout%d" % k, [128, 2048], F32).ap() for k in range(nsc + 1)]
    out = nc.dram_tensor("out", [NT * 128, D], F32, kind="ExternalOutput").ap()
    h1d = nc.dram_tensor("h1d", [NT * 128, D], F32).ap()

    class Arena:
        def __init__(self):
            self.off = SBUF_LO
            self.n = 0

        def alloc(self, name, shape, dt):
            nb = int(np.prod(shape[1:])) * (4 if dt == F32 else 2)
            nb = (nb + 31) // 32 * 32
            assert self.off + nb <= SBUF_HI, ("SBUF overflow", name, self.off, nb)
            h = nc.alloc_sbuf_tensor_at("%s_%d" % (name, self.n), list(shape), dt, offset=self.off)
            self.n += 1
            self.off += nb
            return h

    A = Arena()

    def T(name, shape, dt=F32):
        return A.alloc(name, shape, dt), Buf(name)

    psum = [nc.alloc_psum_tensor("ps%d" % i, [128, 512], F32) for i in range(8)]
    psb = [Buf("ps%d" % i, excl=True) for i in range(8)]

    def mm(bank, out_ap, terms, reads, start=True, stop=True):
        n = len(terms)
        for i, (l, r) in enumerate(terms):
            st = start and i == 0
            sp = stop and i == n - 1
            S.op(PE, (lambda e, l=l, r=r, st=st, sp=sp: e.matmul(out_ap, l, r, start=st, stop=sp)),
                 reads=reads, writes=[psb[bank]], signal=(i == n - 1))

    def dve(fn, reads, writes):
        S.op(DVE, fn, reads, writes)

    def act(fn, reads, writes):
        S.op(ACT, fn, reads, writes)

    def pool(fn, reads, writes):
        S.op(POOL, fn, reads, writes)

    def load(q, dst_ap, src_ap, dstbuf, reads=()):
        S.dma(q, (lambda e: e.dma_start(out=dst_ap, in_=src_ap)), reads, [dstbuf], dstbuf)

    def store(q, dst_ap, src_ap, srcbuf, dstbuf):
        S.dma(q, (lambda e: e.dma_start(out=dst_ap, in_=src_ap)), [srcbuf], [dstbuf], srcbuf)

    cst, cstb = T("cst", [128, 128 * 4 + 512 * 6], BF16)
    load(SP, cst[:], cbf, cstb)
    ident = cst[:, 0:128]
    ntri = cst[:, 128:256]
    ones1 = cst[0:1, 256:384]
    rmaskT = cst[:, 384:512]

    def sbmask(i, Tn):
        return cst[:, 512 + 512 * i: 512 + 512 * i + Tn]

    gam_t, gam_b = T("gam", [128, 16])
    load(SP, gam_t[:], gam, gam_b)
    gch_t, gch_b = T("gch", [128, 1])
    load(SP, gch_t[:], gch, gch_b)

    stg = [T("stg%d" % i, [128, 512]) for i in range(3)]
    stg_i = [0]

    def load_weight(dst, dstbuf, src, nkt, ncols, gcol=None):
        for kt in range(nkt):
            for c0 in range(0, ncols, 512):
                cw = min(512, ncols - c0)
                st, stb = stg[stg_i[0] % 3]
                stg_i[0] += 1
                load(SP, st[:, :cw], src[kt * 128:(kt + 1) * 128, c0:c0 + cw], stb)
                o = dst[:, kt, c0:c0 + cw]
                if gcol is None:
                    pool((lambda e, o=o, st=st, cw=cw: e.tensor_copy(o, st[:, :cw])), [stb], [dstbuf])
                else:
                    g = gam_t[:, gcol + kt:gcol + kt + 1]
                    pool((lambda e, o=o, st=st, cw=cw, g=g: e.tensor_scalar(o, st[:, :cw], g, None, ALU.mult)),
                         [stb, gam_b], [dstbuf])

    def finish():
        with ExitStack() as es:
            sems = {}
            for e_ in ENGS:
                sems[e_] = es.enter_context(nc.semaphore("c_" + e_))
            for dname in S.dsems:
                sems[dname] = es.enter_context(nc.semaphore(dname))
            block = es.enter_context(nc.Block())

            def replay(name, eng):
                for waits, fn, inc in S.ops[name]:
                    for k_, v_ in waits:
                        eng.wait_ge(sems[k_], v_)
                    if fn is None:
                        continue
                    ins = fn(eng)
                    if inc is not None:
                        ins.then_inc(sems[inc[0]], inc[1])

            @block.tensor
            def _(e):
                replay(PE, e)

            @block.scalar
            def _(e):
                replay(ACT, e)

            @block.vector
            def _(e):
                replay(DVE, e)

            @block.gpsimd
            def _(e):
                replay(POOL, e)

            @block.sync
            def _(e):
                replay(SP, e)

        return nc

    hn, hnb = T("hn", [128, D], BF16)
    ss, ssb = T("ss", [128, 2])
    p1_mark = A.off
    wbf, wbf_b = T("wbf", [128, 8, 2304], BF16)
    wrob, wrob_b = T("wrob", [128, 4, D], BF16)
    wsob, wsob_b = T("wsob", [128, 2, D], BF16)
    if phase == 1:
        load_weight(wbf, wbf_b, w1, 8, 2304, gcol=0)
        load_weight(wrob, wrob_b, wro, 4, D)
        load_weight(wsob, wsob_b, wso, 2, D)
    RQ, RK, RV, RG, SQ, SK, SV = 0, 256, 512, 1024, 1536, 1792, 2048

    kT, _ = T("kT", [128, 2, LP], BF16)
    kTb = [[Buf("kT%d_%d" % (p, c)) for c in range(nch)] for p in range(2)]
    vsb, _ = T("vsb", [128, nch, 256], BF16)
    vsbb = [Buf("vsb%d" % c) for c in range(nch)]
    hnT, hnTb = T("hnT", [128, 8, 512], BF16)
    xt = [T("xt%d" % i, [128, D]) for i in range(2)]
    tab, tabb = T("tab", [128, 4, 512])
    qTr, qTrb = T("qTr", [128, 2, 512], BF16)
    kTr, kTrb = T("kTr", [128, 2, 512], BF16)
    qs = [T("qs%d" % i, [128, 2, 512], BF16) for i in range(2)]
    tmp = [T("tmp%d" % i, [128, 512]) for i in range(2)]
    vr = [T("vr%d" % i, [128, 512], BF16) for i in range(2)]
    rgs = [T("rgs%d" % i, [128, 512]) for i in range(2)]
    ktok, ktokb = T("ktok", [128, 256], BF16)
    scm, scmb = T("scm", [128, 128], BF16)
    Rst, Rstb = T("Rst", [128, 2, 512])
    stbf, stbfb = T("stbf", [128, 2, 512], BF16)
    stats, statsb = T("stats", [128, 8])
    gated, gatedb = T("gated", [128, 512], BF16)
    gT, gTb = T("gT", [128, 4, 128], BF16)
    sbT, sbTb = T("sbT", [128, 2, 512], BF16)
    ys, ysb = T("ys", [128, D])
    NE = 3
    Eb = [stg[i] for i in range(3)]
    Sb = [T("S%d" % i, [128, 512], BF16) for i in range(NE)]
    Xb = [T("X%d" % i, [128, 512]) for i in range(2)]
    ab = [T("a%d" % i, [128, 512], BF16) for i in range(NE)]
    crow = [T("crow%d" % i, [1, 512], BF16) for i in range(2)]

    dve(lambda e: e.memset(Rst[:], 0.0), [], [Rstb])
    dve(lambda e: e.memset(stbf[:], 0.0), [], [stbfb])

    pctr = [0]

    def pbank():
        pctr[0] += 1
        return 6 + (pctr[0] % 2)

    def rms_tile(xtile, xbuf, gcols_eps_done=None):
        dve(lambda e: e.memset(ss[:, 0:1], 0.0), [], [ssb])
        act(lambda e: e.activation(hn[:], xtile[:], AF.Square, accum_out=ss[:, 0:1]), [xbuf, ssb], [hnb, ssb])
        act(lambda e: e.activation(ss[:, 1:2], ss[:, 0:1], AF.Ln, bias=1e-6, scale=1.0 / D), [ssb], [ssb])
        act(lambda e: e.activation(ss[:, 1:2], ss[:, 1:2], AF.Exp, scale=-0.5), [ssb], [ssb])

    def norm_and_transpose(xtile, xbuf, dstT, dstTb, col0):
        rms_tile(xtile, xbuf)
        dve(lambda e: e.tensor_scalar(hn[:], xtile[:], ss[:, 1:2], None, ALU.mult), [xbuf, ssb], [hnb])
        transpose_to(hn, hnb, 8, dstT, dstTb, col0)

    def transpose_to(src, srcb, nkt, dstT, dstTb, col0):
        for k0 in range(0, nkt, 4):
            kn = min(4, nkt - k0)
            bk = pbank()
            for kk in range(kn):
                kt = k0 + kk
                mm(bk, psum[bk][:, kk * 128:(kk + 1) * 128], [(src[:, kt * 128:(kt + 1) * 128], ident)],
                   [srcb, cstb])
            for kk in range(kn):
                kt = k0 + kk
                dve(lambda e, bk=bk, kk=kk, kt=kt: e.tensor_copy(dstT[:, kt, col0:col0 + 128],
                                                                 psum[bk][:, kk * 128:(kk + 1) * 128]),
                    [psb[bk]], [dstTb])

    def phase1_proj(k):
        chunks = [0] if k == 0 else list(range(4 * k - 3, 4 * k + 1))
        Tn = 128 * len(chunks)
        tok0 = 128 * chunks[0]
        load(SP, tab[:, :, :Tn], rtab[:, :, tok0:tok0 + Tn].rearrange("a p t -> p a t"), tabb)
        for ci, c in enumerate(chunks):
            x_t, x_b = xt[c % 2]
            load(SP, x_t[:], xall[c * 128:(c + 1) * 128, :], x_b)
            norm_and_transpose(x_t, x_b, hnT, hnTb, ci * 128)

        def fm_proj(col0):
            bk = pbank()
            mm(bk, psum[bk][:, :Tn], [(wbf[:, kt, col0:col0 + 128], hnT[:, kt, :Tn]) for kt in range(8)],
               [wbf_b, hnTb])
            return bk

        for (col, dst, dstb, tc, ts) in ((RQ, qTr, qTrb, 0, 1), (RK, kTr, kTrb, 2, 3)):
            b1 = fm_proj(col)
            b2 = fm_proj(col + 128)
            t0, t0b = tmp[0]
            t1, t1b = tmp[1]
            dve(lambda e, b1=b1, tc=tc: e.tensor_tensor(t0[:, :Tn], psum[b1][:, :Tn], tab[:, tc, :Tn], ALU.mult),
                [psb[b1], tabb], [t0b])
            dve(lambda e, b2=b2, ts=ts: e.tensor_tensor(t1[:, :Tn], psum[b2][:, :Tn], tab[:, ts, :Tn], ALU.mult),
                [psb[b2], tabb], [t1b])
            dve(lambda e, dst=dst: e.tensor_tensor(dst[:, 0, :Tn], t0[:, :Tn], t1[:, :Tn], ALU.subtract),
                [t0b, t1b], [dstb])
            dve(lambda e, b1=b1, ts=ts: e.tensor_tensor(t0[:, :Tn], psum[b1][:, :Tn], tab[:, ts, :Tn], ALU.mult),
                [psb[b1], tabb], [t0b])
            dve(lambda e, b2=b2, tc=tc: e.tensor_tensor(t1[:, :Tn], psum[b2][:, :Tn], tab[:, tc, :Tn], ALU.mult),
                [psb[b2], tabb], [t1b])
            dve(lambda e, dst=dst: e.tensor_tensor(dst[:, 1, :Tn], t0[:, :Tn], t1[:, :Tn], ALU.add),
                [t0b, t1b], [dstb])
        q_t, q_b = qs[k % 2]
        for p in range(2):
            bk = fm_proj(SQ + 128 * p)
            dve(lambda e, bk=bk, p=p: e.tensor_scalar(q_t[:, p, :Tn], psum[bk][:, :Tn], 0.125, None, ALU.mult),
                [psb[bk]], [q_b])
            bk = fm_proj(SK + 128 * p)
            dve(lambda e, bk=bk, p=p: e.tensor_copy(kT[:, p, tok0:tok0 + Tn], psum[bk][:, :Tn]),
                [psb[bk]], [kTb[p][c] for c in chunks])
        for ci, c in enumerate(chunks):
            cs = slice(ci * 128, (ci + 1) * 128)

            def tm_proj(col0, ncols):
                bk = pbank()
                mm(bk, psum[bk][:, :ncols], [(hnT[:, kt, cs], wbf[:, kt, col0:col0 + ncols]) for kt in range(8)],
                   [wbf_b, hnTb])
                return bk

            v_t, v_b = vr[c % 2]
            bk = tm_proj(RV, 512)
            dve(lambda e, bk=bk, v_t=v_t: e.tensor_copy(v_t[:], psum[bk][:]), [psb[bk]], [v_b])
            bk = tm_proj(SV, 256)
            dve(lambda e, bk=bk, c=c: e.tensor_copy(vsb[:, c, :], psum[bk][:, :256]), [psb[bk]], [vsbb[c]])
            rg_t, rg_b = rgs[c % 2]
            if k > 0:
                bk = tm_proj(RG, 512)
                dve(lambda e, bk=bk, rg_t=rg_t: e.tensor_copy(rg_t[:], psum[bk][:]), [psb[bk]], [rg_b])
            bk = pbank()
            for kt in range(2):
                mm(bk, psum[bk][:, kt * 128:(kt + 1) * 128], [(kTr[:, kt, cs], ident)], [kTrb, cstb])
            dve(lambda e, bk=bk: e.tensor_copy(ktok[:], psum[bk][:, :256]), [psb[bk]], [ktokb])
            if k > 0:
                bk = pbank()
                mm(bk, psum[bk][:, :128], [(kTr[:, kt, cs], qTr[:, kt, cs]) for kt in range(2)], [kTrb, qTrb])
                dve(lambda e, bk=bk: e.tensor_tensor(scm[:], psum[bk][:, :128], rmaskT, ALU.mult),
                    [psb[bk], cstb], [scmb])
                bo = pbank()
                mm(bo, psum[bo][:], [(scm[:], v_t[:])] + [(qTr[:, kt, cs], stbf[:, kt, :]) for kt in range(2)],
                   [scmb, v_b, qTrb, stbfb])
            if k == 0:
                for kt in range(2):
                    bk = pbank()
                    mm(bk, psum[bk][:], [(ktok[:, kt * 128:(kt + 1) * 128], v_t[:])], [ktokb, v_b])
                    dve(lambda e, bk=bk, kt=kt: e.scalar_tensor_tensor(Rst[:, kt, :], Rst[:, kt, :], gch_t[:, 0:1],
                                                                        psum[bk][:], ALU.mult, ALU.add),
                        [Rstb, gch_b, psb[bk]], [Rstb])
                dve(lambda e: e.tensor_scalar(stbf[:], Rst[:], gch_t[:, 0:1], None, ALU.mult), [Rstb, gch_b], [stbfb])
                continue
            dve(lambda e, bo=bo: e.bn_stats(stats[:, 0:6], psum[bo][:]), [psb[bo]], [statsb])
            dve(lambda e: e.bn_aggr(stats[:, 6:8], stats[:, 0:6]), [statsb], [statsb])
            act(lambda e: e.activation(stats[:, 7:8], stats[:, 7:8], AF.Ln, bias=1e-5), [statsb], [statsb])
            act(lambda e: e.activation(stats[:, 7:8], stats[:, 7:8], AF.Exp, scale=-0.5), [statsb], [statsb])
            t0, t0b = tmp[0]
            t1, t1b = tmp[1]
            act(lambda e, rg_t=rg_t: e.activation(t0[:], rg_t[:], AF.Exp, scale=-1.0), [rg_b], [t0b])
            dve(lambda e: e.tensor_scalar(t0[:], t0[:], 1.0, None, ALU.add), [t0b], [t0b])
            dve(lambda e: e.reciprocal(t0[:], t0[:]), [t0b], [t0b])
            dve(lambda e, rg_t=rg_t: e.tensor_tensor(t0[:], t0[:], rg_t[:], ALU.mult), [t0b, rg_b], [t0b])
            dve(lambda e, bo=bo: e.tensor_scalar(t1[:], psum[bo][:], stats[:, 6:7], stats[:, 7:8],
                                                 ALU.subtract, ALU.mult), [psb[bo], statsb], [t1b])
            dve(lambda e: e.tensor_tensor(gated[:], t1[:], t0[:], ALU.mult), [t0b, t1b], [gatedb])
            for kt in range(2):
                bk = pbank()
                mm(bk, psum[bk][:], [(ktok[:, kt * 128:(kt + 1) * 128], v_t[:])], [ktokb, v_b])
                dve(lambda e, bk=bk, kt=kt: e.scalar_tensor_tensor(Rst[:, kt, :], Rst[:, kt, :], gch_t[:, 0:1],
                                                                    psum[bk][:], ALU.mult, ALU.add),
                    [Rstb, gch_b, psb[bk]], [Rstb])
            dve(lambda e: e.tensor_scalar(stbf[:], Rst[:], gch_t[:, 0:1], None, ALU.mult), [Rstb, gch_b], [stbfb])
            transpose_to(gated, gatedb, 4, gT, gTb, 0)
            for half in range(2):
                bk = pbank()
                mm(bk, psum[bk][:], [(gT[:, ft, :], wrob[:, ft, half * 512:(half + 1) * 512]) for ft in range(4)],
                   [gTb, wrob_b])
                dve(lambda e, bk=bk, half=half: e.tensor_copy(ys[:, half * 512:(half + 1) * 512], psum[bk][:]),
                    [psb[bk]], [ysb])
            store(SP, rsin[k][cs, 0:D], ys[:], ysb, rsinb[k])

    rsinb = [Buf("rsin%d" % k) for k in range(nsc + 1)]
    rsoutb = [Buf("rsout%d" % k) for k in range(nsc + 1)]

    def phase1_sb(k):
        Tn = 512
        c0 = 4 * k - 3
        q_t, q_b = qs[k % 2]
        n = [0]
        for p in range(2):
            for h in range(2):
                hs = slice(64 * h, 64 * h + 64)
                ob = 4 + h
                first = True
                for j in range(4 * k, -1, -1):
                    i = n[0]
                    n[0] += 1
                    zb = i % 2
                    bb = 2 + (i % 2)
                    E_t, E_b = Eb[i % NE]
                    S_t, S_b = Sb[i % NE]
                    X_t, X_b = Xb[i % 2]
                    a_t, a_b = ab[i % NE]
                    cr_t, cr_b = crow[h]
                    mm(zb, psum[zb][:], [(kT[hs, p, j * 128:(j + 1) * 128], q_t[hs, p, :])], [kTb[p][j], q_b])
                    act(lambda e, zb=zb, E_t=E_t: e.activation(E_t[:], psum[zb][:], AF.Exp), [psb[zb]], [E_b])
                    act(lambda e, E_t=E_t, S_t=S_t: e.activation(S_t[:], E_t[:], AF.Ln, bias=1.0), [E_b], [S_b])
                    mi = None
                    if j >= c0:
                        mi = j - c0
                    elif j == 0:
                        mi = 4
                    if mi is not None:
                        m = sbmask(mi, Tn)
                        dve(lambda e, S_t=S_t, m=m: e.tensor_tensor(S_t[:], S_t[:], m, ALU.mult), [S_b, cstb], [S_b])
                        dve(lambda e, E_t=E_t, m=m: e.tensor_tensor(E_t[:], E_t[:], m, ALU.mult), [E_b, cstb], [E_b])
                    terms = [(ntri, S_t[:])]
                    rd = [S_b, cstb]
                    if not first:
                        terms.append((ones1, cr_t[:]))
                        rd.append(cr_b)
                    mm(bb, psum[bb][:], terms, rd)
                    if j > 0:
                        dve(lambda e, bb=bb, cr_t=cr_t: e.tensor_copy(cr_t[:], psum[bb][0:1, :]), [psb[bb]], [cr_b])
                    act(lambda e, bb=bb, X_t=X_t: e.activation(X_t[:], psum[bb][:], AF.Exp), [psb[bb]], [X_b])
                    dve(lambda e, a_t=a_t, E_t=E_t, X_t=X_t: e.tensor_tensor(a_t[:], E_t[:], X_t[:], ALU.mult),
                        [E_b, X_b], [a_b])
                    mm(ob, psum[ob][:], [(vsb[:, j, 128 * p:128 * p + 128], a_t[:])], [vsbb[j], a_b],
                       start=first, stop=(j == 0))
                    first = False
                dve(lambda e, ob=ob, hs=hs, p=p: e.tensor_copy(sbT[hs, p, :], psum[ob][hs, :]), [psb[ob]], [sbTb])
        for ci in range(4):
            cs = slice(ci * 128, (ci + 1) * 128)
            for half in range(2):
                bk = pbank()
                mm(bk, psum[bk][:], [(sbT[:, p, cs], wsob[:, p, half * 512:(half + 1) * 512]) for p in range(2)],
                   [sbTb, wsob_b])
                dve(lambda e, bk=bk, half=half: e.tensor_copy(ys[:, half * 512:(half + 1) * 512], psum[bk][:]),
                    [psb[bk]], [ysb])
            store(SP, rsin[k][cs, D:2 * D], ys[:], ysb, rsinb[k])

    for k in range(nsc + 1):
        phase1_proj(k)
        if k > 0:
            phase1_sb(k)
            S.dma(POOL, (lambda e, k=k: e.collective_compute("ReduceScatter", ALU.add,
                                                             replica_groups=[[0, 1, 2, 3], [4, 5, 6, 7]],
                                                             ins=[rsin[k].opt()], outs=[rsout[k].opt()])),
                  [rsinb[k]], [rsoutb[k]], rsoutb[k], inc=1)

    def barrier_all():
        evs = [(e_, S.cnt[e_]) for e_ in (PE, ACT, DVE, POOL)] + [(b_.sem, b_.inc * b_.cnt) for b_ in S.owners]
        for eng in ENGS:
            out_ = []
            for k_, v_ in evs:
                if v_ > 0 and S.seen[eng].get(k_, 0) < v_ and not (k_ == eng and eng == PE):
                    S.seen[eng][k_] = v_
                    out_.append((k_, v_))
            S.ops[eng].append((out_, None, None))

    barrier_all()
    A.off = p1_mark
    wgb, wgb_b = T("wgb", [128, 8, 2048], BF16)
    woutb, woutb_b = T("woutb", [128, 8, D], BF16)
    gp_t, gp_b = T("gpost", [128, 2 * D])
    load(SP, gp_t[:], gpost, gp_b)
    load_weight(wgb, wgb_b, wg, 8, 2048, gcol=0)
    load_weight(woutb, woutb_b, wout, 8, D)
    hnT2, hnT2b = T("hnT2", [128, 8, 128], BF16)
    x2 = [T("x2_%d" % i, [128, D]) for i in range(2)]
    yrs = [T("yrs%d" % i, [128, 2 * D]) for i in range(2)]
    sg, sgb = T("sg", [128, 2 * D])
    mg, mgb = T("mg", [128, D])
    mgh, mghb = T("mgh", [128, D], BF16)
    mgT, mgTb = T("mgT", [128, 8, 128], BF16)
    h1, h1b = T("h1", [128, D])
    h1db = [Buf("h1d%d" % i) for i in range(NT)]

    for t in range(NT):
        k = t + 1
        x_t, x_b = x2[t % 2]
        y_t, y_b = yrs[t % 2]
        load(SP, x_t[:], xown[t * 128:(t + 1) * 128, :], x_b)
        load(SP, y_t[:], rsout[k], y_b, reads=[rsoutb[k]])
        norm_and_transpose(x_t, x_b, hnT2, hnT2b, 0)
        for q4 in range(4):
            bk = pbank()
            mm(bk, psum[bk][:], [(hnT2[:, kt, :], wgb[:, kt, q4 * 512:(q4 + 1) * 512]) for kt in range(8)],
               [hnT2b, wgb_b])
            act(lambda e, bk=bk, q4=q4: e.activation(sg[:, q4 * 512:(q4 + 1) * 512], psum[bk][:], AF.Exp, scale=-1.0),
                [psb[bk]], [sgb])
        dve(lambda e: e.tensor_scalar(sg[:], sg[:], 1.0, None, ALU.add), [sgb], [sgb])
        dve(lambda e: e.reciprocal(sg[:], sg[:]), [sgb], [sgb])
        dve(lambda e, y_t=y_t: e.tensor_tensor(sg[:], sg[:], y_t[:], ALU.mult), [sgb, y_b], [sgb])
        dve(lambda e: e.tensor_tensor(mgh[:], sg[:, 0:D], sg[:, D:2 * D], ALU.add), [sgb], [mghb])
        transpose_to(mgh, mghb, 8, mgT, mgTb, 0)
        for half in range(2):
            bk = pbank()
            mm(bk, psum[bk][:], [(mgT[:, kt, :], woutb[:, kt, half * 512:(half + 1) * 512]) for kt in range(8)],
               [mgTb, woutb_b])
            dve(lambda e, bk=bk, half=half: e.tensor_copy(mg[:, half * 512:(half + 1) * 512], psum[bk][:]),
                [psb[bk]], [mgb])
        rms_tile(mg, mgb)
        dve(lambda e: e.tensor_scalar(mg[:], mg[:], ss[:, 1:2], None, ALU.mult), [mgb, ssb], [mgb])
        dve(lambda e: e.tensor_tensor(mg[:], mg[:], gp_t[:, 0:D], ALU.mult), [mgb, gp_b], [mgb])
        dve(lambda e, x_t=x_t: e.tensor_tensor(h1[:], mg[:], x_t[:], ALU.add), [mgb, x_b], [h1b])
        store(SP, h1d[t * 128:(t + 1) * 128, :], h1[:], h1b, h1db[t])

    barrier_all()
    A.off = p1_mark
    gp2_t, gp2_b = T("gpost2", [128, D])
    load(SP, gp2_t[:], gpost[:, D:2 * D], gp2_b)
    wfib, wfib_b = T("wfib", [128, 8, 2 * DFF], BF16)
    wfob, wfob_b = T("wfob", [128, NFT, D], BF16)
    load_weight(wfib, wfib_b, wfi, 8, 2 * DFF, gcol=8)
    load_weight(wfob, wfob_b, wfo, NFT, D)
    hb = [T("hb%d" % i, [128, D]) for i in range(2)]
    hnT3, hnT3b = T("hnT3", [128, 8, 128], BF16)
    uT, uTb = T("uT", [128, NFT, 128], BF16)
    ea, eab = T("ea", [128, 512])
    ff, ffb = T("ff", [128, D])
    ot = [T("ot%d" % i, [128, D]) for i in range(2)]
    outb = [Buf("out%d" % i) for i in range(NT)]

    for t in range(NT):
        h_t, h_b = hb[t % 2]
        o_t, o_b = ot[t % 2]
        load(SP, h_t[:], h1d[t * 128:(t + 1) * 128, :], h_b, reads=[h1db[t]])
        norm_and_transpose(h_t, h_b, hnT3, hnT3b, 0)
        for f0 in range(0, NFT, 4):
            fn_ = min(4, NFT - f0)
            ba = pbank()
            bbk = 4 + (f0 // 4) % 2
            for ff_ in range(fn_):
                f = f0 + ff_
                mm(ba, psum[ba][:, ff_ * 128:(ff_ + 1) * 128],
                   [(wfib[:, kt, f * 128:(f + 1) * 128], hnT3[:, kt, :]) for kt in range(8)], [wfib_b, hnT3b])
            for ff_ in range(fn_):
                f = f0 + ff_
                mm(bbk, psum[bbk][:, ff_ * 128:(ff_ + 1) * 128],
                   [(wfib[:, kt, DFF + f * 128:DFF + (f + 1) * 128], hnT3[:, kt, :]) for kt in range(8)],
                   [wfib_b, hnT3b])
            w_ = fn_ * 128
            act(lambda e, ba=ba, w_=w_: e.activation(ea[:, :w_], psum[ba][:, :w_], AF.Exp, scale=-1.0),
                [psb[ba]], [eab])
            dve(lambda e, w_=w_: e.tensor_scalar(ea[:, :w_], ea[:, :w_], 1.0, None, ALU.add), [eab], [eab])
            dve(lambda e, w_=w_: e.reciprocal(ea[:, :w_], ea[:, :w_]), [eab], [eab])
            dve(lambda e, ba=ba, w_=w_: e.tensor_tensor(ea[:, :w_], ea[:, :w_], psum[ba][:, :w_], ALU.mult),
                [eab, psb[ba]], [eab])
            dve(lambda e, bbk=bbk, w_=w_, f0=f0, fn_=fn_: e.tensor_tensor(
                uT[:, f0:f0 + fn_, :], ea[:, :w_].rearrange("p (f t) -> p f t", t=128),
                psum[bbk][:, :w_].rearrange("p (f t) -> p f t", t=128), ALU.mult),
                [eab, psb[bbk]], [uTb])
        for half in range(2):
            bk = pbank()
            mm(bk, psum[bk][:], [(uT[:, f, :], wfob[:, f, half * 512:(half + 1) * 512]) for f in range(NFT)],
               [uTb, wfob_b])
            dve(lambda e, bk=bk, half=half: e.tensor_copy(ff[:, half * 512:(half + 1) * 512], psum[bk][:]),
                [psb[bk]], [ffb])
        rms_tile(ff, ffb)
        dve(lambda e: e.tensor_scalar(ff[:], ff[:], ss[:, 1:2], None, ALU.mult), [ffb, ssb], [ffb])
        dve(lambda e: e.tensor_tensor(ff[:], ff[:], gp2_t[:], ALU.mult), [ffb, gp2_b], [ffb])
        dve(lambda e, o_t=o_t, h_t=h_t: e.tensor_tensor(o_t[:], ff[:], h_t[:], ALU.add), [ffb, h_b], [o_b])
        store(SP, out[t * 128:(t + 1) * 128, :], o_t[:], o_b, outb[t])
    S.final_waits(SP, outb)

    return finish()


_NC_CACHE = {}


def _consts():
    ident = np.eye(128, dtype=np.float32)
    s = np.arange(128)
    ntri = -(s[:, None] >= s[None, :]).astype(np.float32)
    ones = np.ones((128, 128), np.float32)
    rmaskT = (s[:, None] <= s[None, :]).astype(np.float32)
    t = np.arange(512)
    masks = []
    for d in range(4):
        masks.append(((128 * d + s[:, None]) < t[None, :]).astype(np.float32))
    masks.append(np.broadcast_to((s[:, None] >= PAD), (128, 512)).astype(np.float32))
    masks.append(np.zeros((128, 512), np.float32))
    return np.concatenate([ident, ntri, ones, rmaskT] + masks, axis=1).astype(ml_dtypes.bfloat16)


def kernel(x, meta_tokens, w_in, w_ret_out, w_sb_out, w_out, w_ffn_in, w_ffn_out,
           norm_mix_pre, norm_mix_post, norm_ffn_pre, norm_ffn_post):
    x = np.asarray(x, np.float32)
    B, SEQ, _ = x.shape
    nsc = SEQ // 512
    nch = 1 + 4 * nsc
    LP = 128 * nch
    if nsc not in _NC_CACHE:
        _NC_CACHE[nsc] = build(nsc)
    nc = _NC_CACHE[nsc]
    w_in = np.asarray(w_in, np.float32)[0]
    cb = _consts()
    gam = np.concatenate([np.asarray(norm_mix_pre, np.float32)[0].reshape(8, 128).T,
                          np.asarray(norm_ffn_pre, np.float32)[0].reshape(8, 128).T], axis=1)
    gpost = np.concatenate([np.broadcast_to(np.asarray(norm_mix_post, np.float32)[0], (128, D)),
                            np.broadcast_to(np.asarray(norm_ffn_post, np.float32)[0], (128, D))], axis=1)
    pos = (np.arange(LP, dtype=np.float32) - np.float32(PAD))
    inv = (np.float32(10000.0) ** (-np.arange(128, dtype=np.float32) / np.float32(128))).astype(np.float32)
    ang = (inv[:, None] * pos[None, :]).astype(np.float32)
    cosv, sinv = np.cos(ang).astype(np.float32), np.sin(ang).astype(np.float32)
    loc = (np.arange(LP) % 128).astype(np.float64)
    in_maps = []
    for c in range(8):
        b, g = c // 4, c % 4
        log_g = np.log1p(-(2.0 ** (-5.0 - g)))
        qsc = (np.exp(log_g * (loc + 1.0)) * (256.0 ** -0.5)).astype(np.float32)
        ksc = np.exp(-log_g * (loc + 1.0)).astype(np.float32)
        rtab = np.stack([cosv * qsc, sinv * qsc, cosv * ksc, sinv * ksc]).astype(np.float32)
        gchv = np.full((128, 1), np.exp(log_g * 128.0), np.float32)
        xall = np.concatenate([np.zeros((PAD, D), np.float32), np.asarray(meta_tokens, np.float32), x[b]], axis=0)
        xown = x[b].reshape(nsc, 4, 128, D)[:, g].reshape(nsc * 128, D)
        cols = np.concatenate([
            np.arange(256 * g, 256 * g + 256),
            1024 + np.arange(256 * g, 256 * g + 256),
            2048 + np.arange(512 * g, 512 * g + 512),
            4096 + np.arange(512 * g, 512 * g + 512),
            6144 + np.arange(256 * g, 256 * g + 256),
            7168 + np.arange(256 * g, 256 * g + 256),
            8192 + np.arange(256 * g, 256 * g + 256),
        ])
        in_maps.append({
            "xall": np.ascontiguousarray(xall),
            "xown": np.ascontiguousarray(xown),
            "w1": np.ascontiguousarray(w_in[:, cols]),
            "wg": np.ascontiguousarray(w_in[:, 9216:11264]),
            "wro": np.ascontiguousarray(np.asarray(w_ret_out, np.float32)[0][512 * g:512 * g + 512]),
            "wso": np.ascontiguousarray(np.asarray(w_sb_out, np.float32)[0][256 * g:256 * g + 256]),
            "wout": np.ascontiguousarray(np.asarray(w_out, np.float32)[0]),
            "wfi": np.ascontiguousarray(np.asarray(w_ffn_in, np.float32)[0]),
            "wfo": np.ascontiguousarray(np.asarray(w_ffn_out, np.float32)[0]),
            "gam": np.ascontiguousarray(gam),
            "gpost": np.ascontiguousarray(gpost),
            "rtab": np.ascontiguousarray(rtab),
            "gch": gchv,
            "cbf": cb,
        })
    res = run_bass_kernel_spmd(nc, in_maps, core_ids=list(range(8)))
    outp = np.zeros((B, SEQ, D), np.float32)
    for c in range(8):
        b, g = c // 4, c % 4
        o = np.asarray(res.results[c]["out"], np.float32).reshape(nsc, 128, D)
        outp[b].reshape(nsc, 4, 128, D)[:, g] = o
    return outp
```

```python
from contextlib import ExitStack
import numpy as np
import ml_dtypes
import concourse.bass as bass
import concourse.mybir as mybir
from concourse.bass_utils import run_bass_kernel_spmd

F32 = mybir.dt.float32
BF16 = mybir.dt.bfloat16
AF = mybir.ActivationFunctionType
ALU = mybir.AluOpType

D = 1024
NMETA = 16
PAD = 112
DFF = 2816
NFT = DFF // 128
PE, ACT, DVE, POOL, SP = "pe", "act", "dve", "pool", "sp"
ENGS = [PE, ACT, DVE, POOL, SP]
SBUF_LO = 16512
SBUF_HI = 229376
import os
NO_CC = bool(int(os.environ.get('NO_CC', '0')))


class Buf:
    __slots__ = ("name", "w", "r", "sem", "cnt", "excl", "inc")

    def __init__(self, name, excl=False):
        self.name = name
        self.excl = excl
        self.w = None
        self.r = {}
        self.sem = None
        self.cnt = 0


class Sched:
    def __init__(self):
        self.ops = {e: [] for e in ENGS}
        self.cnt = {e: 0 for e in ENGS}
        self.seen = {e: {} for e in ENGS}
        self.dsems = []
        self.owners = []

    def _waits(self, eng, reads, writes):
        ev = []
        for b in reads:
            if b.w is not None:
                ev.append(b.w)
        for b in writes:
            if b.w is not None:
                ev.append(b.w)
            ev.extend(b.r.items())
        out = {}
        for k, v in ev:
            if k == eng and eng == PE:
                continue
            if self.seen[eng].get(k, 0) >= v:
                continue
            out[k] = max(out.get(k, 0), v)
        for k, v in out.items():
            self.seen[eng][k] = v
        return list(out.items())

    def op(self, eng, fn, reads=(), writes=(), signal=True):
        writes = list(writes) + [b for b in reads if b.excl and b not in writes]
        waits = self._waits(eng, reads, writes)
        if signal:
            self.cnt[eng] += 1
            v = self.cnt[eng]
        else:
            v = self.cnt[eng] + 1
        for b in reads:
            b.r[eng] = max(b.r.get(eng, 0), v)
        for b in writes:
            b.w = (eng, v)
            b.r = {}
        self.ops[eng].append((waits, fn, (eng, 1) if signal else None))

    def dma(self, q, fn, reads, writes, owner, inc=16):
        waits = self._waits(q, reads, writes)
        if owner.sem is None:
            owner.sem = "d%d" % len(self.dsems)
            self.dsems.append(owner.sem)
            self.owners.append(owner)
        owner.cnt += 1
        ev = (owner.sem, inc * owner.cnt)
        for b in reads:
            b.r[ev[0]] = max(b.r.get(ev[0], 0), ev[1])
        for b in writes:
            b.w = ev
            b.r = {}
        self.ops[q].append((waits, fn, (owner.sem, inc)))
        owner.inc = inc

    def final_waits(self, eng, bufs):
        waits = self._waits(eng, bufs, ())
        self.ops[eng].append((waits, None, None))


def build(nsc):
    phase = 1
    nch = 1 + 4 * nsc
    LP = 128 * nch
    NT = nsc
    nc = bass.Bass("TRN2", target_bir_lowering=False)
    S = Sched()

    def din(name, shape, dt=F32):
        return nc.dram_tensor(name, list(shape), dt, kind="ExternalInput").ap()

    xall = din("xall", [LP, D])
    xown = din("xown", [NT * 128, D])
    w1 = din("w1", [D, 2304])
    wg = din("wg", [D, 2048])
    wro = din("wro", [512, D])
    wso = din("wso", [256, D])
    wout = din("wout", [D, D])
    wfi = din("wfi", [D, 2 * DFF])
    wfo = din("wfo", [DFF, D])
    gam = din("gam", [128, 16])
    gpost = din("gpost", [128, 2 * D])
    rtab = din("rtab", [4, 128, LP])
    gch = din("gch", [128, 1])
    cbf = din("cbf", [128, 128 * 4 + 512 * 6], BF16)
    rsin = [nc.dram_tensor("rsin%d" % k, [512, 2048], F32).ap() for k in range(nsc + 1)]
    rsout = [nc.dram_tensor("rsout%d" % k, [128, 2048], F32).ap() for k in range(nsc + 1)]
    out = nc.dram_tensor("out", [NT * 128, D], F32, kind="ExternalOutput").ap()
    h1d = nc.dram_tensor("h1d", [NT * 128, D], F32).ap()

    class Arena:
        def __init__(self):
            self.off = SBUF_LO
            self.n = 0

        def alloc(self, name, shape, dt):
            nb = int(np.prod(shape[1:])) * (4 if dt == F32 else 2)
            nb = (nb + 31) // 32 * 32
            assert self.off + nb <= SBUF_HI, ("SBUF overflow", name, self.off, nb)
            h = nc.alloc_sbuf_tensor_at("%s_%d" % (name, self.n), list(shape), dt, offset=self.off)
            self.n += 1
            self.off += nb
            return h

    A = Arena()

    def T(name, shape, dt=F32):
        return A.alloc(name, shape, dt), Buf(name)

    psum = [nc.alloc_psum_tensor("ps%d" % i, [128, 512], F32) for i in range(8)]
    psb = [Buf("ps%d" % i, excl=True) for i in range(8)]

    def mm(bank, out_ap, terms, reads, start=True, stop=True):
        n = len(terms)
        for i, (l, r) in enumerate(terms):
            st = start and i == 0
            sp = stop and i == n - 1
            S.op(PE, (lambda e, l=l, r=r, st=st, sp=sp: e.matmul(out_ap, l, r, start=st, stop=sp)),
                 reads=reads, writes=[psb[bank]], signal=(i == n - 1))

    def dve(fn, reads, writes):
        S.op(DVE, fn, reads, writes)

    def act(fn, reads, writes):
        S.op(ACT, fn, reads, writes)

    def pool(fn, reads, writes):
        S.op(POOL, fn, reads, writes)

    def load(q, dst_ap, src_ap, dstbuf, reads=()):
        S.dma(q, (lambda e: e.dma_start(out=dst_ap, in_=src_ap)), reads, [dstbuf], dstbuf)

    def store(q, dst_ap, src_ap, srcbuf, dstbuf):
        S.dma(q, (lambda e: e.dma_start(out=dst_ap, in_=src_ap)), [srcbuf], [dstbuf], srcbuf)

    cst, cstb = T("cst", [128, 128 * 4 + 512 * 6], BF16)
    load(SP, cst[:], cbf, cstb)
    ident = cst[:, 0:128]
    ntri = cst[:, 128:256]
    ones1 = cst[0:1, 256:384]
    rmaskT = cst[:, 384:512]

    def sbmask(i, Tn):
        return cst[:, 512 + 512 * i: 512 + 512 * i + Tn]

    gam_t, gam_b = T("gam", [128, 16])
    load(SP, gam_t[:], gam, gam_b)
    gch_t, gch_b = T("gch", [128, 1])
    load(SP, gch_t[:], gch, gch_b)

    stg = [T("stg%d" % i, [128, 512]) for i in range(3)]
    stg_i = [0]

    def load_weight(dst, dstbuf, src, nkt, ncols, gcol=None):
        for kt in range(nkt):
            for c0 in range(0, ncols, 512):
                cw = min(512, ncols - c0)
                st, stb = stg[stg_i[0] % 3]
                stg_i[0] += 1
                load(SP, st[:, :cw], src[kt * 128:(kt + 1) * 128, c0:c0 + cw], stb)
                o = dst[:, kt, c0:c0 + cw]
                if gcol is None:
                    pool((lambda e, o=o, st=st, cw=cw: e.tensor_copy(o, st[:, :cw])), [stb], [dstbuf])
                else:
                    g = gam_t[:, gcol + kt:gcol + kt + 1]
                    pool((lambda e, o=o, st=st, cw=cw, g=g: e.tensor_scalar(o, st[:, :cw], g, None, ALU.mult)),
                         [stb, gam_b], [dstbuf])

    def finish():
        with ExitStack() as es:
            sems = {}
            for e_ in ENGS:
                sems[e_] = es.enter_context(nc.semaphore("c_" + e_))
            for dname in S.dsems:
                sems[dname] = es.enter_context(nc.semaphore(dname))
            block = es.enter_context(nc.Block())

            def replay(name, eng):
                for waits, fn, inc in S.ops[name]:
                    for k_, v_ in waits:
                        eng.wait_ge(sems[k_], v_)
                    if fn is None:
                        continue
                    ins = fn(eng)
                    if inc is not None:
                        ins.then_inc(sems[inc[0]], inc[1])

            @block.tensor
            def _(e):
                replay(PE, e)

            @block.scalar
            def _(e):
                replay(ACT, e)

            @block.vector
            def _(e):
                replay(DVE, e)

            @block.gpsimd
            def _(e):
                replay(POOL, e)

            @block.sync
            def _(e):
                replay(SP, e)

        return nc

    hn, hnb = T("hn", [128, D], BF16)
    ss, ssb = T("ss", [128, 2])
    p1_mark = A.off
    wbf, wbf_b = T("wbf", [128, 8, 2304], BF16)
    wrob, wrob_b = T("wrob", [128, 4, D], BF16)
    wsob, wsob_b = T("wsob", [128, 2, D], BF16)
    if phase == 1:
        load_weight(wbf, wbf_b, w1, 8, 2304, gcol=0)
        load_weight(wrob, wrob_b, wro, 4, D)
        load_weight(wsob, wsob_b, wso, 2, D)
    RQ, RK, RV, RG, SQ, SK, SV = 0, 256, 512, 1024, 1536, 1792, 2048

    kT, _ = T("kT", [128, 2, LP], BF16)
    kTb = [[Buf("kT%d_%d" % (p, c)) for c in range(nch)] for p in range(2)]
    vsb, _ = T("vsb", [128, nch, 256], BF16)
    vsbb = [Buf("vsb%d" % c) for c in range(nch)]
    hnT, hnTb = T("hnT", [128, 8, 512], BF16)
    xt = [T("xt%d" % i, [128, D]) for i in range(2)]
    tab, tabb = T("tab", [128, 4, 512])
    qTr, qTrb = T("qTr", [128, 2, 512], BF16)
    kTr, kTrb = T("kTr", [128, 2, 512], BF16)
    qs = [T("qs%d" % i, [128, 2, 512], BF16) for i in range(2)]
    tmp = [T("tmp%d" % i, [128, 512]) for i in range(2)]
    vr = [T("vr%d" % i, [128, 512], BF16) for i in range(2)]
    rgs = [T("rgs%d" % i, [128, 512]) for i in range(2)]
    ktok, ktokb = T("ktok", [128, 256], BF16)
    scm, scmb = T("scm", [128, 128], BF16)
    Rst, Rstb = T("Rst", [128, 2, 512])
    stbf, stbfb = T("stbf", [128, 2, 512], BF16)
    stats, statsb = T("stats", [128, 8])
    gated, gatedb = T("gated", [128, 512], BF16)
    gT, gTb = T("gT", [128, 4, 128], BF16)
    sbT, sbTb = T("sbT", [128, 2, 512], BF16)
    ys, ysb = T("ys", [128, D])
    NE = 3
    Eb = [stg[i] for i in range(3)]
    Sb = [T("S%d" % i, [128, 512], BF16) for i in range(NE)]
    Xb = [T("X%d" % i, [128, 512]) for i in range(2)]
    ab = [T("a%d" % i, [128, 512], BF16) for i in range(NE)]
    crow = [T("crow%d" % i, [1, 512], BF16) for i in range(2)]

    dve(lambda e: e.memset(Rst[:], 0.0), [], [Rstb])
    dve(lambda e: e.memset(stbf[:], 0.0), [], [stbfb])

    pctr = [0]

    def pbank():
        pctr[0] += 1
        return 6 + (pctr[0] % 2)

    def rms_tile(xtile, xbuf, gcols_eps_done=None):
        dve(lambda e: e.memset(ss[:, 0:1], 0.0), [], [ssb])
        act(lambda e: e.activation(hn[:], xtile[:], AF.Square, accum_out=ss[:, 0:1]), [xbuf, ssb], [hnb, ssb])
        act(lambda e: e.activation(ss[:, 1:2], ss[:, 0:1], AF.Ln, bias=1e-6, scale=1.0 / D), [ssb], [ssb])
        act(lambda e: e.activation(ss[:, 1:2], ss[:, 1:2], AF.Exp, scale=-0.5), [ssb], [ssb])

    def norm_and_transpose(xtile, xbuf, dstT, dstTb, col0):
        rms_tile(xtile, xbuf)
        dve(lambda e: e.tensor_scalar(hn[:], xtile[:], ss[:, 1:2], None, ALU.mult), [xbuf, ssb], [hnb])
        transpose_to(hn, hnb, 8, dstT, dstTb, col0)

    def transpose_to(src, srcb, nkt, dstT, dstTb, col0):
        for k0 in range(0, nkt, 4):
            kn = min(4, nkt - k0)
            bk = pbank()
            for kk in range(kn):
                kt = k0 + kk
                mm(bk, psum[bk][:, kk * 128:(kk + 1) * 128], [(src[:, kt * 128:(kt + 1) * 128], ident)],
                   [srcb, cstb])
            for kk in range(kn):
                kt = k0 + kk
                dve(lambda e, bk=bk, kk=kk, kt=kt: e.tensor_copy(dstT[:, kt, col0:col0 + 128],
                                                                 psum[bk][:, kk * 128:(kk + 1) * 128]),
                    [psb[bk]], [dstTb])

    def phase1_proj(k):
        chunks = [0] if k == 0 else list(range(4 * k - 3, 4 * k + 1))
        Tn = 128 * len(chunks)
        tok0 = 128 * chunks[0]
        load(SP, tab[:, :, :Tn], rtab[:, :, tok0:tok0 + Tn].rearrange("a p t -> p a t"), tabb)
        for ci, c in enumerate(chunks):
            x_t, x_b = xt[c % 2]
            load(SP, x_t[:], xall[c * 128:(c + 1) * 128, :], x_b)
            norm_and_transpose(x_t, x_b, hnT, hnTb, ci * 128)
            yield

        def fm_proj(col0):
            bk = pbank()
            mm(bk, psum[bk][:, :Tn], [(wbf[:, kt, col0:col0 + 128], hnT[:, kt, :Tn]) for kt in range(8)],
               [wbf_b, hnTb])
            return bk

        for (col, dst, dstb, tc, ts) in ((RQ, qTr, qTrb, 0, 1), (RK, kTr, kTrb, 2, 3)):
            b1 = fm_proj(col)
            b2 = fm_proj(col + 128)
            t0, t0b = tmp[0]
            t1, t1b = tmp[1]
            dve(lambda e, b1=b1, tc=tc: e.tensor_tensor(t0[:, :Tn], psum[b1][:, :Tn], tab[:, tc, :Tn], ALU.mult),
                [psb[b1], tabb], [t0b])
            dve(lambda e, b2=b2, ts=ts: e.tensor_tensor(t1[:, :Tn], psum[b2][:, :Tn], tab[:, ts, :Tn], ALU.mult),
                [psb[b2], tabb], [t1b])
            dve(lambda e, dst=dst: e.tensor_tensor(dst[:, 0, :Tn], t0[:, :Tn], t1[:, :Tn], ALU.subtract),
                [t0b, t1b], [dstb])
            yield
            dve(lambda e, b1=b1, ts=ts: e.tensor_tensor(t0[:, :Tn], psum[b1][:, :Tn], tab[:, ts, :Tn], ALU.mult),
                [psb[b1], tabb], [t0b])
            dve(lambda e, b2=b2, tc=tc: e.tensor_tensor(t1[:, :Tn], psum[b2][:, :Tn], tab[:, tc, :Tn], ALU.mult),
                [psb[b2], tabb], [t1b])
            dve(lambda e, dst=dst: e.tensor_tensor(dst[:, 1, :Tn], t0[:, :Tn], t1[:, :Tn], ALU.add),
                [t0b, t1b], [dstb])
            yield
        q_t, q_b = qs[k % 2]
        for p in range(2):
            bk = fm_proj(SQ + 128 * p)
            dve(lambda e, bk=bk, p=p: e.tensor_scalar(q_t[:, p, :Tn], psum[bk][:, :Tn], 0.125, None, ALU.mult),
                [psb[bk]], [q_b])
            bk = fm_proj(SK + 128 * p)
            dve(lambda e, bk=bk, p=p: e.tensor_copy(kT[:, p, tok0:tok0 + Tn], psum[bk][:, :Tn]),
                [psb[bk]], [kTb[p][c] for c in chunks])
            yield
        for ci, c in enumerate(chunks):
            cs = slice(ci * 128, (ci + 1) * 128)

            def tm_proj(col0, ncols):
                bk = pbank()
                mm(bk, psum[bk][:, :ncols], [(hnT[:, kt, cs], wbf[:, kt, col0:col0 + ncols]) for kt in range(8)],
                   [wbf_b, hnTb])
                return bk

            v_t, v_b = vr[c % 2]
            bk = tm_proj(RV, 512)
            dve(lambda e, bk=bk, v_t=v_t: e.tensor_copy(v_t[:], psum[bk][:]), [psb[bk]], [v_b])
            yield
            bk = tm_proj(SV, 256)
            dve(lambda e, bk=bk, c=c: e.tensor_copy(vsb[:, c, :], psum[bk][:, :256]), [psb[bk]], [vsbb[c]])
            yield
            rg_t, rg_b = rgs[c % 2]
            if k > 0:
                bk = tm_proj(RG, 512)
                dve(lambda e, bk=bk, rg_t=rg_t: e.tensor_copy(rg_t[:], psum[bk][:]), [psb[bk]], [rg_b])
            bk = pbank()
            for kt in range(2):
                mm(bk, psum[bk][:, kt * 128:(kt + 1) * 128], [(kTr[:, kt, cs], ident)], [kTrb, cstb])
            dve(lambda e, bk=bk: e.tensor_copy(ktok[:], psum[bk][:, :256]), [psb[bk]], [ktokb])
            yield
            if k > 0:
                bk = pbank()
                mm(bk, psum[bk][:, :128], [(kTr[:, kt, cs], qTr[:, kt, cs]) for kt in range(2)], [kTrb, qTrb])
                dve(lambda e, bk=bk: e.tensor_tensor(scm[:], psum[bk][:, :128], rmaskT, ALU.mult),
                    [psb[bk], cstb], [scmb])
                bo = pbank()
                mm(bo, psum[bo][:], [(scm[:], v_t[:])] + [(qTr[:, kt, cs], stbf[:, kt, :]) for kt in range(2)],
                   [scmb, v_b, qTrb, stbfb])
            if k == 0:
                for kt in range(2):
                    bk = pbank()
                    mm(bk, psum[bk][:], [(ktok[:, kt * 128:(kt + 1) * 128], v_t[:])], [ktokb, v_b])
                    dve(lambda e, bk=bk, kt=kt: e.scalar_tensor_tensor(Rst[:, kt, :], Rst[:, kt, :], gch_t[:, 0:1],
                                                                        psum[bk][:], ALU.mult, ALU.add),
                        [Rstb, gch_b, psb[bk]], [Rstb])
                dve(lambda e: e.tensor_scalar(stbf[:], Rst[:], gch_t[:, 0:1], None, ALU.mult), [Rstb, gch_b], [stbfb])
                continue
            dve(lambda e, bo=bo: e.bn_stats(stats[:, 0:6], psum[bo][:]), [psb[bo]], [statsb])
            dve(lambda e: e.bn_aggr(stats[:, 6:8], stats[:, 0:6]), [statsb], [statsb])
            act(lambda e: e.activation(stats[:, 7:8], stats[:, 7:8], AF.Ln, bias=1e-5), [statsb], [statsb])
            act(lambda e: e.activation(stats[:, 7:8], stats[:, 7:8], AF.Exp, scale=-0.5), [statsb], [statsb])
            t0, t0b = tmp[0]
            t1, t1b = tmp[1]
            act(lambda e, rg_t=rg_t: e.activation(t0[:], rg_t[:], AF.Exp, scale=-1.0), [rg_b], [t0b])
            dve(lambda e: e.tensor_scalar(t0[:], t0[:], 1.0, None, ALU.add), [t0b], [t0b])
            dve(lambda e: e.reciprocal(t0[:], t0[:]), [t0b], [t0b])
            dve(lambda e, rg_t=rg_t: e.tensor_tensor(t0[:], t0[:], rg_t[:], ALU.mult), [t0b, rg_b], [t0b])
            dve(lambda e, bo=bo: e.tensor_scalar(t1[:], psum[bo][:], stats[:, 6:7], stats[:, 7:8],
                                                 ALU.subtract, ALU.mult), [psb[bo], statsb], [t1b])
            dve(lambda e: e.tensor_tensor(gated[:], t1[:], t0[:], ALU.mult), [t0b, t1b], [gatedb])
            yield
            for kt in range(2):
                bk = pbank()
                mm(bk, psum[bk][:], [(ktok[:, kt * 128:(kt + 1) * 128], v_t[:])], [ktokb, v_b])
                dve(lambda e, bk=bk, kt=kt: e.scalar_tensor_tensor(Rst[:, kt, :], Rst[:, kt, :], gch_t[:, 0:1],
                                                                    psum[bk][:], ALU.mult, ALU.add),
                    [Rstb, gch_b, psb[bk]], [Rstb])
            dve(lambda e: e.tensor_scalar(stbf[:], Rst[:], gch_t[:, 0:1], None, ALU.mult), [Rstb, gch_b], [stbfb])
            transpose_to(gated, gatedb, 4, gT, gTb, 0)
            yield
            for half in range(2):
                bk = pbank()
                mm(bk, psum[bk][:], [(gT[:, ft, :], wrob[:, ft, half * 512:(half + 1) * 512]) for ft in range(4)],
                   [gTb, wrob_b])
                dve(lambda e, bk=bk, half=half: e.tensor_copy(ys[:, half * 512:(half + 1) * 512], psum[bk][:]),
                    [psb[bk]], [ysb])
            store(SP, rsin[k][cs, 0:D], ys[:], ysb, rsinb[k])
            yield

    rsinb = [Buf("rsin%d" % k) for k in range(nsc + 1)]
    rsoutb = [Buf("rsout%d" % k) for k in range(nsc + 1)]

    def phase1_sb(k, filler=None):
        Tn = 512
        c0 = 4 * k - 3
        q_t, q_b = qs[k % 2]
        tiles = []
        for p in range(2):
            for j in range(4 * k, -1, -1):
                for h in range(2):
                    tiles.append((p, j, h, len(tiles)))
        N = len(tiles)

        def res(t):
            p, j, h, i = t
            return dict(p=p, j=j, h=h, hs=slice(64 * h, 64 * h + 64), ob=4 + h, zb=i % 2, bb=2 + (i % 2),
                        E=Eb[i % NE], S=Sb[i % NE], X=Xb[i % 2], a=ab[i % NE], cr=crow[h],
                        first=(j == 4 * k), last=(j == 0))

        def s0(t):
            r = res(t)
            mm(r["zb"], psum[r["zb"]][:], [(kT[r["hs"], r["p"], r["j"] * 128:(r["j"] + 1) * 128], q_t[r["hs"], r["p"], :])],
               [kTb[r["p"]][r["j"]], q_b])

        def s1(t):
            r = res(t)
            zb = r["zb"]
            E_t, E_b = r["E"]
            S_t, S_b = r["S"]
            act(lambda e: e.activation(E_t[:], psum[zb][:], AF.Exp), [psb[zb]], [E_b])
            act(lambda e: e.activation(S_t[:], E_t[:], AF.Ln, bias=1.0), [E_b], [S_b])
            j = r["j"]
            mi = None
            if j >= c0:
                mi = j - c0
            elif j == 0:
                mi = 4
            if mi is not None:
                m = sbmask(mi, Tn)
                dve(lambda e: e.tensor_tensor(S_t[:], S_t[:], m, ALU.mult), [S_b, cstb], [S_b])
                dve(lambda e: e.tensor_tensor(E_t[:], E_t[:], m, ALU.mult), [E_b, cstb], [E_b])

        def s2(t):
            r = res(t)
            S_t, S_b = r["S"]
            cr_t, cr_b = r["cr"]
            terms = [(ntri, S_t[:])]
            rd = [S_b, cstb]
            if not r["first"]:
                terms.append((ones1, cr_t[:]))
                rd.append(cr_b)
            mm(r["bb"], psum[r["bb"]][:], terms, rd)

        def s3(t):
            r = res(t)
            bb = r["bb"]
            cr_t, cr_b = r["cr"]
            E_t, E_b = r["E"]
            X_t, X_b = r["X"]
            a_t, a_b = r["a"]
            if not r["last"]:
                dve(lambda e: e.tensor_copy(cr_t[:], psum[bb][0:1, :]), [psb[bb]], [cr_b])
            act(lambda e: e.activation(X_t[:], psum[bb][:], AF.Exp), [psb[bb]], [X_b])
            dve(lambda e: e.tensor_tensor(a_t[:], E_t[:], X_t[:], ALU.mult), [E_b, X_b], [a_b])

        def s4(t):
            r = res(t)
            a_t, a_b = r["a"]
            ob, hs, p, j = r["ob"], r["hs"], r["p"], r["j"]
            mm(ob, psum[ob][:], [(vsb[:, j, 128 * p:128 * p + 128], a_t[:])], [vsbb[j], a_b],
               start=r["first"], stop=r["last"])
            if r["last"]:
                dve(lambda e: e.tensor_copy(sbT[hs, p, :], psum[ob][hs, :]), [psb[ob]], [sbTb])

        for st in range(N + 4):
            if 0 <= st - 3 < N:
                s3(tiles[st - 3])
            if 0 <= st - 2 < N:
                s2(tiles[st - 2])
            if 0 <= st - 4 < N:
                s4(tiles[st - 4])
            if 0 <= st - 1 < N:
                s1(tiles[st - 1])
            if st < N:
                s0(tiles[st])
            if filler is not None:
                next(filler, None)
        if filler is not None:
            for _ in filler:
                pass
        for ci in range(4):
            cs = slice(ci * 128, (ci + 1) * 128)
            for half in range(2):
                bk = pbank()
                mm(bk, psum[bk][:], [(sbT[:, p, cs], wsob[:, p, half * 512:(half + 1) * 512]) for p in range(2)],
                   [sbTb, wsob_b])
                dve(lambda e, bk=bk, half=half: e.tensor_copy(ys[:, half * 512:(half + 1) * 512], psum[bk][:]),
                    [psb[bk]], [ysb])
            store(SP, rsin[k][cs, D:2 * D], ys[:], ysb, rsinb[k])

    for _ in phase1_proj(0):
        pass
    for _ in phase1_proj(1):
        pass
    for k in range(1, nsc + 1):
        if True:
            phase1_sb(k, phase1_proj(k + 1) if k < nsc else None)
            S.dma(POOL, (lambda e, k=k: e.collective_compute("ReduceScatter", ALU.add,
                                                             replica_groups=[[0, 1, 2, 3], [4, 5, 6, 7]],
                                                             ins=[rsin[k].opt()], outs=[rsout[k].opt()])),
                  [rsinb[k]], [rsoutb[k]], rsoutb[k], inc=1)

    def barrier_all():
        evs = [(e_, S.cnt[e_]) for e_ in (PE, ACT, DVE, POOL)] + [(b_.sem, b_.inc * b_.cnt) for b_ in S.owners]
        for eng in ENGS:
            out_ = []
            for k_, v_ in evs:
                if v_ > 0 and S.seen[eng].get(k_, 0) < v_ and not (k_ == eng and eng == PE):
                    S.seen[eng][k_] = v_
                    out_.append((k_, v_))
            S.ops[eng].append((out_, None, None))

    barrier_all()
    A.off = p1_mark
    wgb, wgb_b = T("wgb", [128, 8, 2048], BF16)
    woutb, woutb_b = T("woutb", [128, 8, D], BF16)
    gp_t, gp_b = T("gpost", [128, 2 * D])
    load(SP, gp_t[:], gpost, gp_b)
    load_weight(wgb, wgb_b, wg, 8, 2048, gcol=0)
    load_weight(woutb, woutb_b, wout, 8, D)
    hnT2, hnT2b = T("hnT2", [128, 8, 128], BF16)
    x2 = [T("x2_%d" % i, [128, D]) for i in range(2)]
    yrs = [T("yrs%d" % i, [128, 2 * D]) for i in range(2)]
    sg, sgb = T("sg", [128, 2 * D])
    mg, mgb = T("mg", [128, D])
    mgh, mghb = T("mgh", [128, D], BF16)
    mgT, mgTb = T("mgT", [128, 8, 128], BF16)
    h1, h1b = T("h1", [128, D])
    h1db = [Buf("h1d%d" % i) for i in range(NT)]

    for t in range(NT):
        k = t + 1
        x_t, x_b = x2[t % 2]
        y_t, y_b = yrs[t % 2]
        load(SP, x_t[:], xown[t * 128:(t + 1) * 128, :], x_b)
        load(SP, y_t[:], rsout[k], y_b, reads=[rsoutb[k]])
        norm_and_transpose(x_t, x_b, hnT2, hnT2b, 0)
        for q4 in range(4):
            bk = pbank()
            mm(bk, psum[bk][:], [(hnT2[:, kt, :], wgb[:, kt, q4 * 512:(q4 + 1) * 512]) for kt in range(8)],
               [hnT2b, wgb_b])
            act(lambda e, bk=bk, q4=q4: e.activation(sg[:, q4 * 512:(q4 + 1) * 512], psum[bk][:], AF.Exp, scale=-1.0),
                [psb[bk]], [sgb])
        dve(lambda e: e.tensor_scalar(sg[:], sg[:], 1.0, None, ALU.add), [sgb], [sgb])
        dve(lambda e: e.reciprocal(sg[:], sg[:]), [sgb], [sgb])
        dve(lambda e, y_t=y_t: e.tensor_tensor(sg[:], sg[:], y_t[:], ALU.mult), [sgb, y_b], [sgb])
        dve(lambda e: e.tensor_tensor(mgh[:], sg[:, 0:D], sg[:, D:2 * D], ALU.add), [sgb], [mghb])
        transpose_to(mgh, mghb, 8, mgT, mgTb, 0)
        for half in range(2):
            bk = pbank()
            mm(bk, psum[bk][:], [(mgT[:, kt, :], woutb[:, kt, half * 512:(half + 1) * 512]) for kt in range(8)],
               [mgTb, woutb_b])
            dve(lambda e, bk=bk, half=half: e.tensor_copy(mg[:, half * 512:(half + 1) * 512], psum[bk][:]),
                [psb[bk]], [mgb])
        rms_tile(mg, mgb)
        dve(lambda e: e.tensor_scalar(mg[:], mg[:], ss[:, 1:2], None, ALU.mult), [mgb, ssb], [mgb])
        dve(lambda e: e.tensor_tensor(mg[:], mg[:], gp_t[:, 0:D], ALU.mult), [mgb, gp_b], [mgb])
        dve(lambda e, x_t=x_t: e.tensor_tensor(h1[:], mg[:], x_t[:], ALU.add), [mgb, x_b], [h1b])
        store(SP, h1d[t * 128:(t + 1) * 128, :], h1[:], h1b, h1db[t])

    barrier_all()
    A.off = p1_mark
    gp2_t, gp2_b = T("gpost2", [128, D])
    load(SP, gp2_t[:], gpost[:, D:2 * D], gp2_b)
    wfib, wfib_b = T("wfib", [128, 8, 2 * DFF], BF16)
    wfob, wfob_b = T("wfob", [128, NFT, D], BF16)
    load_weight(wfib, wfib_b, wfi, 8, 2 * DFF, gcol=8)
    load_weight(wfob, wfob_b, wfo, NFT, D)
    hb = [T("hb%d" % i, [128, D]) for i in range(2)]
    hnT3, hnT3b = T("hnT3", [128, 8, 128], BF16)
    uT, uTb = T("uT", [128, NFT, 128], BF16)
    ea, eab = T("ea", [128, 512])
    ff, ffb = T("ff", [128, D])
    ot = [T("ot%d" % i, [128, D]) for i in range(2)]
    outb = [Buf("out%d" % i) for i in range(NT)]

    for t in range(NT):
        h_t, h_b = hb[t % 2]
        o_t, o_b = ot[t % 2]
        load(SP, h_t[:], h1d[t * 128:(t + 1) * 128, :], h_b, reads=[h1db[t]])
        norm_and_transpose(h_t, h_b, hnT3, hnT3b, 0)
        for f0 in range(0, NFT, 4):
            fn_ = min(4, NFT - f0)
            ba = pbank()
            bbk = 4 + (f0 // 4) % 2
            for ff_ in range(fn_):
                f = f0 + ff_
                mm(ba, psum[ba][:, ff_ * 128:(ff_ + 1) * 128],
                   [(wfib[:, kt, f * 128:(f + 1) * 128], hnT3[:, kt, :]) for kt in range(8)], [wfib_b, hnT3b])
            for ff_ in range(fn_):
                f = f0 + ff_
                mm(bbk, psum[bbk][:, ff_ * 128:(ff_ + 1) * 128],
                   [(wfib[:, kt, DFF + f * 128:DFF + (f + 1) * 128], hnT3[:, kt, :]) for kt in range(8)],
                   [wfib_b, hnT3b])
            w_ = fn_ * 128
            act(lambda e, ba=ba, w_=w_: e.activation(ea[:, :w_], psum[ba][:, :w_], AF.Exp, scale=-1.0),
                [psb[ba]], [eab])
            dve(lambda e, w_=w_: e.tensor_scalar(ea[:, :w_], ea[:, :w_], 1.0, None, ALU.add), [eab], [eab])
            dve(lambda e, w_=w_: e.reciprocal(ea[:, :w_], ea[:, :w_]), [eab], [eab])
            dve(lambda e, ba=ba, w_=w_: e.tensor_tensor(ea[:, :w_], ea[:, :w_], psum[ba][:, :w_], ALU.mult),
                [eab, psb[ba]], [eab])
            dve(lambda e, bbk=bbk, w_=w_, f0=f0, fn_=fn_: e.tensor_tensor(
                uT[:, f0:f0 + fn_, :], ea[:, :w_].rearrange("p (f t) -> p f t", t=128),
                psum[bbk][:, :w_].rearrange("p (f t) -> p f t", t=128), ALU.mult),
                [eab, psb[bbk]], [uTb])
        for half in range(2):
            bk = pbank()
            mm(bk, psum[bk][:], [(uT[:, f, :], wfob[:, f, half * 512:(half + 1) * 512]) for f in range(NFT)],
               [uTb, wfob_b])
            dve(lambda e, bk=bk, half=half: e.tensor_copy(ff[:, half * 512:(half + 1) * 512], psum[bk][:]),
                [psb[bk]], [ffb])
        rms_tile(ff, ffb)
        dve(lambda e: e.tensor_scalar(ff[:], ff[:], ss[:, 1:2], None, ALU.mult), [ffb, ssb], [ffb])
        dve(lambda e: e.tensor_tensor(ff[:], ff[:], gp2_t[:], ALU.mult), [ffb, gp2_b], [ffb])
        dve(lambda e, o_t=o_t, h_t=h_t: e.tensor_tensor(o_t[:], ff[:], h_t[:], ALU.add), [ffb, h_b], [o_b])
        store(SP, out[t * 128:(t + 1) * 128, :], o_t[:], o_b, outb[t])
    S.final_waits(SP, outb)

    return finish()


_NC_CACHE = {}


def _consts():
    ident = np.eye(128, dtype=np.float32)
    s = np.arange(128)
    ntri = -(s[:, None] >= s[None, :]).astype(np.float32)
    ones = np.ones((128, 128), np.float32)
    rmaskT = (s[:, None] <= s[None, :]).astype(np.float32)
    t = np.arange(512)
    masks = []
    for d in range(4):
        masks.append(((128 * d + s[:, None]) < t[None, :]).astype(np.float32))
    masks.append(np.broadcast_to((s[:, None] >= PAD), (128, 512)).astype(np.float32))
    masks.append(np.zeros((128, 512), np.float32))
    return np.concatenate([ident, ntri, ones, rmaskT] + masks, axis=1).astype(ml_dtypes.bfloat16)


def kernel(x, meta_tokens, w_in, w_ret_out, w_sb_out, w_out, w_ffn_in, w_ffn_out,
           norm_mix_pre, norm_mix_post, norm_ffn_pre, norm_ffn_post):
    x = np.asarray(x, np.float32)
    B, SEQ, _ = x.shape
    nsc = SEQ // 512
    nch = 1 + 4 * nsc
    LP = 128 * nch
    if nsc not in _NC_CACHE:
        _NC_CACHE[nsc] = build(nsc)
    nc = _NC_CACHE[nsc]
    w_in = np.asarray(w_in, np.float32)[0]
    cb = _consts()
    gam = np.concatenate([np.asarray(norm_mix_pre, np.float32)[0].reshape(8, 128).T,
                          np.asarray(norm_ffn_pre, np.float32)[0].reshape(8, 128).T], axis=1)
    gpost = np.concatenate([np.broadcast_to(np.asarray(norm_mix_post, np.float32)[0], (128, D)),
                            np.broadcast_to(np.asarray(norm_ffn_post, np.float32)[0], (128, D))], axis=1)
    pos = (np.arange(LP, dtype=np.float32) - np.float32(PAD))
    inv = (np.float32(10000.0) ** (-np.arange(128, dtype=np.float32) / np.float32(128))).astype(np.float32)
    ang = (inv[:, None] * pos[None, :]).astype(np.float32)
    cosv, sinv = np.cos(ang).astype(np.float32), np.sin(ang).astype(np.float32)
    loc = (np.arange(LP) % 128).astype(np.float64)
    in_maps = []
    for c in range(8):
        b, g = c // 4, c % 4
        log_g = np.log1p(-(2.0 ** (-5.0 - g)))
        qsc = (np.exp(log_g * (loc + 1.0)) * (256.0 ** -0.5)).astype(np.float32)
        ksc = np.exp(-log_g * (loc + 1.0)).astype(np.float32)
        rtab = np.stack([cosv * qsc, sinv * qsc, cosv * ksc, sinv * ksc]).astype(np.float32)
        gchv = np.full((128, 1), np.exp(log_g * 128.0), np.float32)
        xall = np.concatenate([np.zeros((PAD, D), np.float32), np.asarray(meta_tokens, np.float32), x[b]], axis=0)
        xown = x[b].reshape(nsc, 4, 128, D)[:, g].reshape(nsc * 128, D)
        cols = np.concatenate([
            np.arange(256 * g, 256 * g + 256),
            1024 + np.arange(256 * g, 256 * g + 256),
            2048 + np.arange(512 * g, 512 * g + 512),
            4096 + np.arange(512 * g, 512 * g + 512),
            6144 + np.arange(256 * g, 256 * g + 256),
            7168 + np.arange(256 * g, 256 * g + 256),
            8192 + np.arange(256 * g, 256 * g + 256),
        ])
        in_maps.append({
            "xall": np.ascontiguousarray(xall),
            "xown": np.ascontiguousarray(xown),
            "w1": np.ascontiguousarray(w_in[:, cols]),
            "wg": np.ascontiguousarray(w_in[:, 9216:11264]),
            "wro": np.ascontiguousarray(np.asarray(w_ret_out, np.float32)[0][512 * g:512 * g + 512]),
            "wso": np.ascontiguousarray(np.asarray(w_sb_out, np.float32)[0][256 * g:256 * g + 256]),
            "wout": np.ascontiguousarray(np.asarray(w_out, np.float32)[0]),
            "wfi": np.ascontiguousarray(np.asarray(w_ffn_in, np.float32)[0]),
            "wfo": np.ascontiguousarray(np.asarray(w_ffn_out, np.float32)[0]),
            "gam": np.ascontiguousarray(gam),
            "gpost": np.ascontiguousarray(gpost),
            "rtab": np.ascontiguousarray(rtab),
            "gch": gchv,
            "cbf": cb,
        })
    res = run_bass_kernel_spmd(nc, in_maps, core_ids=list(range(8)))
    outp = np.zeros((B, SEQ, D), np.float32)
    for c in range(8):
        b, g = c // 4, c % 4
        o = np.asarray(res.results[c]["out"], np.float32).reshape(nsc, 128, D)
        outp[b].reshape(nsc, 4, 128, D)[:, g] = o
    return outp
```

```python
from contextlib import ExitStack
import numpy as np
import ml_dtypes
import concourse.bass as bass
import concourse.mybir as mybir
from concourse.bass_utils import run_bass_kernel_spmd

F32 = mybir.dt.float32
BF16 = mybir.dt.bfloat16
AF = mybir.ActivationFunctionType
ALU = mybir.AluOpType

D = 1024
NMETA = 16
PAD = 112
DFF = 2816
NFT = DFF // 128
PE, ACT, DVE, POOL, SP = "pe", "act", "dve", "pool", "sp"
ENGS = [PE, ACT, DVE, POOL, SP]
SBUF_LO = 16512
SBUF_HI = 229376
import os
NO_CC = bool(int(os.environ.get('NO_CC', '0')))


class Buf:
    __slots__ = ("name", "w", "r", "sem", "cnt", "excl", "inc")

    def __init__(self, name, excl=False):
        self.name = name
        self.excl = excl
        self.w = None
        self.r = {}
        self.sem = None
        self.cnt = 0


class Sched:
    def __init__(self):
        self.ops = {e: [] for e in ENGS}
        self.cnt = {e: 0 for e in ENGS}
        self.seen = {e: {} for e in ENGS}
        self.dsems = []
        self.owners = []

    def _waits(self, eng, reads, writes):
        ev = []
        for b in reads:
            if b.w is not None:
                ev.append(b.w)
        for b in writes:
            if b.w is not None:
                ev.append(b.w)
            ev.extend(b.r.items())
        out = {}
        for k, v in ev:
            if k == eng and eng == PE:
                continue
            if self.seen[eng].get(k, 0) >= v:
                continue
            out[k] = max(out.get(k, 0), v)
        for k, v in out.items():
            self.seen[eng][k] = v
        return list(out.items())

    def op(self, eng, fn, reads=(), writes=(), signal=True):
        writes = list(writes) + [b for b in reads if b.excl and b not in writes]
        waits = self._waits(eng, reads, writes)
        if signal:
            self.cnt[eng] += 1
            v = self.cnt[eng]
        else:
            v = self.cnt[eng] + 1
        for b in reads:
            b.r[eng] = max(b.r.get(eng, 0), v)
        for b in writes:
            b.w = (eng, v)
            b.r = {}
        self.ops[eng].append((waits, fn, (eng, 1) if signal else None))

    def dma(self, q, fn, reads, writes, owner, inc=16):
        waits = self._waits(q, reads, writes)
        if owner.sem is None:
            owner.sem = "d%d" % len(self.dsems)
            self.dsems.append(owner.sem)
            self.owners.append(owner)
        owner.cnt += 1
        ev = (owner.sem, inc * owner.cnt)
        for b in reads:
            b.r[ev[0]] = max(b.r.get(ev[0], 0), ev[1])
        for b in writes:
            b.w = ev
            b.r = {}
        self.ops[q].append((waits, fn, (owner.sem, inc)))
        owner.inc = inc

    def final_waits(self, eng, bufs):
        waits = self._waits(eng, bufs, ())
        self.ops[eng].append((waits, None, None))


def build(nsc):
    phase = 1
    nch = 1 + 4 * nsc
    LP = 128 * nch
    NT = nsc
    nc = bass.Bass("TRN2", target_bir_lowering=False)
    S = Sched()

    def din(name, shape, dt=F32):
        return nc.dram_tensor(name, list(shape), dt, kind="ExternalInput").ap()

    xall = din("xall", [LP, D])
    xown = din("xown", [NT * 128, D])
    w1 = din("w1", [D, 2304])
    wg = din("wg", [D, 2048])
    wro = din("wro", [512, D])
    wso = din("wso", [256, D])
    wout = din("wout", [D, D])
    wfi = din("wfi", [D, 2 * DFF])
    wfo = din("wfo", [DFF, D])
    gam = din("gam", [128, 16])
    gpost = din("gpost", [128, 2 * D])
    rtab = din("rtab", [4, 128, LP])
    gch = din("gch", [128, 1])
    cbf = din("cbf", [128, 128 * 4 + 512 * 6], BF16)
    rsin = [nc.dram_tensor("rsin%d" % k, [512, 2048], F32).ap() for k in range(nsc + 1)]
    rsout = [nc.dram_tensor("rsout%d" % k, [128, 2048], F32).ap() for k in range(nsc + 1)]
    out = nc.dram_tensor("out", [NT * 128, D], F32, kind="ExternalOutput").ap()
    h1d = nc.dram_tensor("h1d", [NT * 128, D], F32).ap()
    wg_d = nc.dram_tensor("wg_d", [D, 2048], BF16).ap()
    wout_d = nc.dram_tensor("wout_d", [D, D], BF16).ap()
    wfi_d = nc.dram_tensor("wfi_d", [D, 2 * DFF], BF16).ap()
    wfo_d = nc.dram_tensor("wfo_d", [DFF, D], BF16).ap()

    class Arena:
        def __init__(self):
            self.off = SBUF_LO
            self.n = 0

        def alloc(self, name, shape, dt):
            nb = int(np.prod(shape[1:])) * (4 if dt == F32 else 2)
            nb = (nb + 31) // 32 * 32
            assert self.off + nb <= SBUF_HI, ("SBUF overflow", name, self.off, nb)
            h = nc.alloc_sbuf_tensor_at("%s_%d" % (name, self.n), list(shape), dt, offset=self.off)
            self.n += 1
            self.off += nb
            return h

    A = Arena()

    def T(name, shape, dt=F32):
        return A.alloc(name, shape, dt), Buf(name)

    psum = [nc.alloc_psum_tensor("ps%d" % i, [128, 512], F32) for i in range(8)]
    psb = [Buf("ps%d" % i, excl=True) for i in range(8)]

    DEFER = [None]

    def emit(thunk):
        if DEFER[0] is not None:
            DEFER[0].append(thunk)
        else:
            thunk()

    def mm(bank, out_ap, terms, reads, start=True, stop=True):
        reads = list(reads)

        def go():
            n = len(terms)
            for i, (l, r) in enumerate(terms):
                st = start and i == 0
                sp = stop and i == n - 1
                S.op(PE, (lambda e, l=l, r=r, st=st, sp=sp: e.matmul(out_ap, l, r, start=st, stop=sp)),
                     reads=reads, writes=[psb[bank]], signal=(i == n - 1))
        emit(go)

    def dve(fn, reads, writes):
        reads, writes = list(reads), list(writes)
        emit(lambda: S.op(DVE, fn, reads, writes))

    def act(fn, reads, writes):
        reads, writes = list(reads), list(writes)
        emit(lambda: S.op(ACT, fn, reads, writes))

    def pool(fn, reads, writes):
        reads, writes = list(reads), list(writes)
        emit(lambda: S.op(POOL, fn, reads, writes))

    def load(q, dst_ap, src_ap, dstbuf, reads=()):
        reads = list(reads)
        emit(lambda: S.dma(q, (lambda e: e.dma_start(out=dst_ap, in_=src_ap)), reads, [dstbuf], dstbuf))

    def store(q, dst_ap, src_ap, srcbuf, dstbuf):
        emit(lambda: S.dma(q, (lambda e: e.dma_start(out=dst_ap, in_=src_ap)), [srcbuf], [dstbuf], srcbuf))

    cst, cstb = T("cst", [128, 128 * 4 + 512 * 6], BF16)
    load(SP, cst[:], cbf, cstb)
    ident = cst[:, 0:128]
    ntri = cst[:, 128:256]
    ones1 = cst[0:1, 256:384]
    rmaskT = cst[:, 384:512]

    def sbmask(i, Tn):
        return cst[:, 512 + 512 * i: 512 + 512 * i + Tn]

    gam_t, gam_b = T("gam", [128, 16])
    load(SP, gam_t[:], gam, gam_b)
    gch_t, gch_b = T("gch", [128, 1])
    load(SP, gch_t[:], gch, gch_b)

    stg = [T("stg%d" % i, [128, 512]) for i in range(3)]
    stg_i = [0]

    def load_weight(dst, dstbuf, src, nkt, ncols, gcol=None):
        for kt in range(nkt):
            for c0 in range(0, ncols, 512):
                cw = min(512, ncols - c0)
                st, stb = stg[stg_i[0] % 3]
                stg_i[0] += 1
                load(SP, st[:, :cw], src[kt * 128:(kt + 1) * 128, c0:c0 + cw], stb)
                o = dst[:, kt, c0:c0 + cw]
                if gcol is None:
                    pool((lambda e, o=o, st=st, cw=cw: e.tensor_copy(o, st[:, :cw])), [stb], [dstbuf])
                else:
                    g = gam_t[:, gcol + kt:gcol + kt + 1]
                    pool((lambda e, o=o, st=st, cw=cw, g=g: e.tensor_scalar(o, st[:, :cw], g, None, ALU.mult)),
                         [stb, gam_b], [dstbuf])

    def finish():
        with ExitStack() as es:
            sems = {}
            for e_ in ENGS:
                sems[e_] = es.enter_context(nc.semaphore("c_" + e_))
            for dname in S.dsems:
                sems[dname] = es.enter_context(nc.semaphore(dname))
            block = es.enter_context(nc.Block())

            def replay(name, eng):
                for waits, fn, inc in S.ops[name]:
                    for k_, v_ in waits:
                        eng.wait_ge(sems[k_], v_)
                    if fn is None:
                        continue
                    ins = fn(eng)
                    if inc is not None:
                        ins.then_inc(sems[inc[0]], inc[1])

            @block.tensor
            def _(e):
                replay(PE, e)

            @block.scalar
            def _(e):
                replay(ACT, e)

            @block.vector
            def _(e):
                replay(DVE, e)

            @block.gpsimd
            def _(e):
                replay(POOL, e)

            @block.sync
            def _(e):
                replay(SP, e)

        return nc

    hn, hnb = T("hn", [128, D], BF16)
    ss, ssb = T("ss", [128, 2])
    p1_mark = A.off
    wbf, wbf_b = T("wbf", [128, 8, 2304], BF16)
    wrob, wrob_b = T("wrob", [128, 4, D], BF16)
    wsob, wsob_b = T("wsob", [128, 2, D], BF16)
    if phase == 1:
        load_weight(wbf, wbf_b, w1, 8, 2304, gcol=0)
        load_weight(wrob, wrob_b, wro, 4, D)
        load_weight(wsob, wsob_b, wso, 2, D)
    RQ, RK, RV, RG, SQ, SK, SV = 0, 256, 512, 1024, 1536, 1792, 2048

    kT, _ = T("kT", [128, 2, LP], BF16)
    kTb = [[Buf("kT%d_%d" % (p, c)) for c in range(nch)] for p in range(2)]
    vsb, _ = T("vsb", [128, nch, 256], BF16)
    vsbb = [Buf("vsb%d" % c) for c in range(nch)]
    hnT, hnTb = T("hnT", [128, 8, 512], BF16)
    xt = [T("xt%d" % i, [128, D]) for i in range(2)]
    tab, tabb = T("tab", [128, 4, 512])
    qTr, qTrb = T("qTr", [128, 2, 512], BF16)
    kTr, kTrb = T("kTr", [128, 2, 512], BF16)
    qs = [T("qs%d" % i, [128, 2, 512], BF16) for i in range(2)]
    tmp = [T("tmp%d" % i, [128, 512]) for i in range(2)]
    vr = [T("vr%d" % i, [128, 512], BF16) for i in range(2)]
    rgs = [T("rgs%d" % i, [128, 512]) for i in range(2)]
    ktok, ktokb = T("ktok", [128, 256], BF16)
    scm, scmb = T("scm", [128, 128], BF16)
    Rst, Rstb = T("Rst", [128, 2, 512])
    stbf, stbfb = T("stbf", [128, 2, 512], BF16)
    stats, statsb = T("stats", [128, 8])
    gated, gatedb = T("gated", [128, 512], BF16)
    gT, gTb = T("gT", [128, 4, 128], BF16)
    sbT, sbTb = T("sbT", [128, 2, 512], BF16)
    ys, ysb = T("ys", [128, D])
    NE = 3
    Eb = [stg[i] for i in range(3)]
    Sb = [T("S%d" % i, [128, 512], BF16) for i in range(NE)]
    Xb = [T("X%d" % i, [128, 512]) for i in range(2)]
    ab = [T("a%d" % i, [128, 512], BF16) for i in range(NE)]
    crow = [T("crow%d" % i, [1, 512], BF16) for i in range(2)]

    pst = [T("pst%d" % i, [128, 512]) for i in range(2)]
    pob = [T("pob%d" % i, [128, 512], BF16) for i in range(2)]
    precast_bufs = {}

    def precast(name, dst, src, nkt, ncols, gcol=None):
        db = Buf("pc_" + name)
        precast_bufs[name] = db
        pieces = [(kt, c0, min(512, ncols - c0)) for kt in range(nkt) for c0 in range(0, ncols, 512)]

        def L(i):
            kt, c0, cw = pieces[i]
            st, stb = pst[i % 2]
            load(POOL, st[:, :cw], src[kt * 128:(kt + 1) * 128, c0:c0 + cw], stb)

        def C(i):
            kt, c0, cw = pieces[i]
            st, stb = pst[i % 2]
            ob_, obb = pob[i % 2]
            if gcol is None:
                pool((lambda e: e.tensor_copy(ob_[:, :cw], st[:, :cw])), [stb], [obb])
            else:
                g = gam_t[:, gcol + kt:gcol + kt + 1]
                pool((lambda e: e.tensor_scalar(ob_[:, :cw], st[:, :cw], g, None, ALU.mult)), [stb, gam_b], [obb])
            store(POOL, dst[kt * 128:(kt + 1) * 128, c0:c0 + cw], ob_[:, :cw], obb, db)

        L(0)
        for i in range(len(pieces)):
            if i + 1 < len(pieces):
                L(i + 1)
            C(i)

    precast("wg", wg_d, wg, 8, 2048, gcol=0)
    precast("wout", wout_d, wout, 8, D)
    precast("wfi", wfi_d, wfi, 8, 2 * DFF, gcol=8)
    precast("wfo", wfo_d, wfo, NFT, D)

    dve(lambda e: e.memset(Rst[:], 0.0), [], [Rstb])
    dve(lambda e: e.memset(stbf[:], 0.0), [], [stbfb])

    pctr = [0]

    def pbank():
        pctr[0] += 1
        return 6 + (pctr[0] % 2)

    def rms_tile(xtile, xbuf, gcols_eps_done=None):
        dve(lambda e: e.memset(ss[:, 0:1], 0.0), [], [ssb])
        act(lambda e: e.activation(hn[:], xtile[:], AF.Square, accum_out=ss[:, 0:1]), [xbuf, ssb], [hnb, ssb])
        act(lambda e: e.activation(ss[:, 1:2], ss[:, 0:1], AF.Ln, bias=1e-6, scale=1.0 / D), [ssb], [ssb])
        act(lambda e: e.activation(ss[:, 1:2], ss[:, 1:2], AF.Exp, scale=-0.5), [ssb], [ssb])

    def norm_and_transpose(xtile, xbuf, dstT, dstTb, col0):
        rms_tile(xtile, xbuf)
        dve(lambda e: e.tensor_scalar(hn[:], xtile[:], ss[:, 1:2], None, ALU.mult), [xbuf, ssb], [hnb])
        transpose_to(hn, hnb, 8, dstT, dstTb, col0)

    def transpose_to(src, srcb, nkt, dstT, dstTb, col0):
        for k0 in range(0, nkt, 4):
            kn = min(4, nkt - k0)
            bk = pbank()
            for kk in range(kn):
                kt = k0 + kk
                mm(bk, psum[bk][:, kk * 128:(kk + 1) * 128], [(src[:, kt * 128:(kt + 1) * 128], ident)],
                   [srcb, cstb])
            for kk in range(kn):
                kt = k0 + kk
                dve(lambda e, bk=bk, kk=kk, kt=kt: e.tensor_copy(dstT[:, kt, col0:col0 + 128],
                                                                 psum[bk][:, kk * 128:(kk + 1) * 128]),
                    [psb[bk]], [dstTb])

    def phase1_proj(k):
        chunks = [0] if k == 0 else list(range(4 * k - 3, 4 * k + 1))
        Tn = 128 * len(chunks)
        tok0 = 128 * chunks[0]
        load(SP, tab[:, :, :Tn], rtab[:, :, tok0:tok0 + Tn].rearrange("a p t -> p a t"), tabb)
        for ci, c in enumerate(chunks):
            x_t, x_b = xt[c % 2]
            load(SP, x_t[:], xall[c * 128:(c + 1) * 128, :], x_b)
            norm_and_transpose(x_t, x_b, hnT, hnTb, ci * 128)
            yield

        def fm_proj(col0):
            bk = pbank()
            mm(bk, psum[bk][:, :Tn], [(wbf[:, kt, col0:col0 + 128], hnT[:, kt, :Tn]) for kt in range(8)],
               [wbf_b, hnTb])
            return bk

        for (col, dst, dstb, tc, ts) in ((RQ, qTr, qTrb, 0, 1), (RK, kTr, kTrb, 2, 3)):
            b1 = fm_proj(col)
            b2 = fm_proj(col + 128)
            t0, t0b = tmp[0]
            t1, t1b = tmp[1]
            dve(lambda e, b1=b1, tc=tc: e.tensor_tensor(t0[:, :Tn], psum[b1][:, :Tn], tab[:, tc, :Tn], ALU.mult),
                [psb[b1], tabb], [t0b])
            dve(lambda e, b2=b2, ts=ts: e.tensor_tensor(t1[:, :Tn], psum[b2][:, :Tn], tab[:, ts, :Tn], ALU.mult),
                [psb[b2], tabb], [t1b])
            dve(lambda e, dst=dst: e.tensor_tensor(dst[:, 0, :Tn], t0[:, :Tn], t1[:, :Tn], ALU.subtract),
                [t0b, t1b], [dstb])
            yield
            dve(lambda e, b1=b1, ts=ts: e.tensor_tensor(t0[:, :Tn], psum[b1][:, :Tn], tab[:, ts, :Tn], ALU.mult),
                [psb[b1], tabb], [t0b])
            dve(lambda e, b2=b2, tc=tc: e.tensor_tensor(t1[:, :Tn], psum[b2][:, :Tn], tab[:, tc, :Tn], ALU.mult),
                [psb[b2], tabb], [t1b])
            dve(lambda e, dst=dst: e.tensor_tensor(dst[:, 1, :Tn], t0[:, :Tn], t1[:, :Tn], ALU.add),
                [t0b, t1b], [dstb])
            yield
        q_t, q_b = qs[k % 2]
        for p in range(2):
            bk = fm_proj(SQ + 128 * p)
            dve(lambda e, bk=bk, p=p: e.tensor_scalar(q_t[:, p, :Tn], psum[bk][:, :Tn], 0.125, None, ALU.mult),
                [psb[bk]], [q_b])
            bk = fm_proj(SK + 128 * p)
            dve(lambda e, bk=bk, p=p: e.tensor_copy(kT[:, p, tok0:tok0 + Tn], psum[bk][:, :Tn]),
                [psb[bk]], [kTb[p][c] for c in chunks])
            yield
        for ci, c in enumerate(chunks):
            cs = slice(ci * 128, (ci + 1) * 128)

            def tm_proj(col0, ncols):
                bk = pbank()
                mm(bk, psum[bk][:, :ncols], [(hnT[:, kt, cs], wbf[:, kt, col0:col0 + ncols]) for kt in range(8)],
                   [wbf_b, hnTb])
                return bk

            v_t, v_b = vr[c % 2]
            bk = tm_proj(RV, 512)
            dve(lambda e, bk=bk, v_t=v_t: e.tensor_copy(v_t[:], psum[bk][:]), [psb[bk]], [v_b])
            yield
            bk = tm_proj(SV, 256)
            dve(lambda e, bk=bk, c=c: e.tensor_copy(vsb[:, c, :], psum[bk][:, :256]), [psb[bk]], [vsbb[c]])
            yield
            rg_t, rg_b = rgs[c % 2]
            if k > 0:
                bk = tm_proj(RG, 512)
                dve(lambda e, bk=bk, rg_t=rg_t: e.tensor_copy(rg_t[:], psum[bk][:]), [psb[bk]], [rg_b])
            bk = pbank()
            for kt in range(2):
                mm(bk, psum[bk][:, kt * 128:(kt + 1) * 128], [(kTr[:, kt, cs], ident)], [kTrb, cstb])
            dve(lambda e, bk=bk: e.tensor_copy(ktok[:], psum[bk][:, :256]), [psb[bk]], [ktokb])
            yield
            if k > 0:
                bk = pbank()
                mm(bk, psum[bk][:, :128], [(kTr[:, kt, cs], qTr[:, kt, cs]) for kt in range(2)], [kTrb, qTrb])
                dve(lambda e, bk=bk: e.tensor_tensor(scm[:], psum[bk][:, :128], rmaskT, ALU.mult),
                    [psb[bk], cstb], [scmb])
                bo = pbank()
                mm(bo, psum[bo][:], [(scm[:], v_t[:])] + [(qTr[:, kt, cs], stbf[:, kt, :]) for kt in range(2)],
                   [scmb, v_b, qTrb, stbfb])
            if k == 0:
                for kt in range(2):
                    bk = pbank()
                    mm(bk, psum[bk][:], [(ktok[:, kt * 128:(kt + 1) * 128], v_t[:])], [ktokb, v_b])
                    dve(lambda e, bk=bk, kt=kt: e.scalar_tensor_tensor(Rst[:, kt, :], Rst[:, kt, :], gch_t[:, 0:1],
                                                                        psum[bk][:], ALU.mult, ALU.add),
                        [Rstb, gch_b, psb[bk]], [Rstb])
                dve(lambda e: e.tensor_scalar(stbf[:], Rst[:], gch_t[:, 0:1], None, ALU.mult), [Rstb, gch_b], [stbfb])
                continue
            dve(lambda e, bo=bo: e.bn_stats(stats[:, 0:6], psum[bo][:]), [psb[bo]], [statsb])
            dve(lambda e: e.bn_aggr(stats[:, 6:8], stats[:, 0:6]), [statsb], [statsb])
            act(lambda e: e.activation(stats[:, 7:8], stats[:, 7:8], AF.Ln, bias=1e-5), [statsb], [statsb])
            act(lambda e: e.activation(stats[:, 7:8], stats[:, 7:8], AF.Exp, scale=-0.5), [statsb], [statsb])
            t0, t0b = tmp[0]
            t1, t1b = tmp[1]
            act(lambda e, rg_t=rg_t: e.activation(t0[:], rg_t[:], AF.Exp, scale=-1.0), [rg_b], [t0b])
            act(lambda e: e.activation(t0[:], t0[:], AF.Ln, bias=1.0), [t0b], [t0b])
            act(lambda e: e.activation(t0[:], t0[:], AF.Exp, scale=-1.0), [t0b], [t0b])
            dve(lambda e, rg_t=rg_t: e.tensor_tensor(t0[:], t0[:], rg_t[:], ALU.mult), [t0b, rg_b], [t0b])
            dve(lambda e, bo=bo: e.tensor_scalar(t1[:], psum[bo][:], stats[:, 6:7], stats[:, 7:8],
                                                 ALU.subtract, ALU.mult), [psb[bo], statsb], [t1b])
            dve(lambda e: e.tensor_tensor(gated[:], t1[:], t0[:], ALU.mult), [t0b, t1b], [gatedb])
            yield
            for kt in range(2):
                bk = pbank()
                mm(bk, psum[bk][:], [(ktok[:, kt * 128:(kt + 1) * 128], v_t[:])], [ktokb, v_b])
                dve(lambda e, bk=bk, kt=kt: e.scalar_tensor_tensor(Rst[:, kt, :], Rst[:, kt, :], gch_t[:, 0:1],
                                                                    psum[bk][:], ALU.mult, ALU.add),
                    [Rstb, gch_b, psb[bk]], [Rstb])
            dve(lambda e: e.tensor_scalar(stbf[:], Rst[:], gch_t[:, 0:1], None, ALU.mult), [Rstb, gch_b], [stbfb])
            transpose_to(gated, gatedb, 4, gT, gTb, 0)
            yield
            for half in range(2):
                bk = pbank()
                mm(bk, psum[bk][:], [(gT[:, ft, :], wrob[:, ft, half * 512:(half + 1) * 512]) for ft in range(4)],
                   [gTb, wrob_b])
                dve(lambda e, bk=bk, half=half: e.tensor_copy(ys[:, half * 512:(half + 1) * 512], psum[bk][:]),
                    [psb[bk]], [ysb])
            store(SP, rsin[k][cs, 0:D], ys[:], ysb, rsinb[k])
            yield

    rsinb = [Buf("rsin%d" % k) for k in range(nsc + 1)]
    rsoutb = [Buf("rsout%d" % k) for k in range(nsc + 1)]

    def phase1_sb(k, filler=None):
        Tn = 512
        c0 = 4 * k - 3
        q_t, q_b = qs[k % 2]
        tiles = []
        for p in range(2):
            for j in range(4 * k, -1, -1):
                for h in range(2):
                    tiles.append((p, j, h, len(tiles)))
        N = len(tiles)
        filler = filler or []
        fpos = [0]
        rate = -(-len(filler) // max(1, N - 8))

        def res(t):
            p, j, h, i = t
            return dict(p=p, j=j, h=h, hs=slice(64 * h, 64 * h + 64), ob=4 + h, zb=i % 2, bb=2 + (i % 2),
                        E=Eb[i % NE], S=Sb[i % NE], X=Xb[i % 2], a=ab[i % NE], cr=crow[h],
                        first=(j == 4 * k), last=(j == 0))

        def s0(t):
            r = res(t)
            mm(r["zb"], psum[r["zb"]][:], [(kT[r["hs"], r["p"], r["j"] * 128:(r["j"] + 1) * 128], q_t[r["hs"], r["p"], :])],
               [kTb[r["p"]][r["j"]], q_b])

        def s1(t):
            r = res(t)
            zb = r["zb"]
            E_t, E_b = r["E"]
            S_t, S_b = r["S"]
            act(lambda e: e.activation(E_t[:], psum[zb][:], AF.Exp), [psb[zb]], [E_b])
            act(lambda e: e.activation(S_t[:], E_t[:], AF.Ln, bias=1.0), [E_b], [S_b])
            j = r["j"]
            mi = None
            if j >= c0:
                mi = j - c0
            elif j == 0:
                mi = 4
            if mi is not None:
                m = sbmask(mi, Tn)
                dve(lambda e: e.tensor_tensor(S_t[:], S_t[:], m, ALU.mult), [S_b, cstb], [S_b])
                dve(lambda e: e.tensor_tensor(E_t[:], E_t[:], m, ALU.mult), [E_b, cstb], [E_b])

        def s2(t):
            r = res(t)
            S_t, S_b = r["S"]
            cr_t, cr_b = r["cr"]
            terms = [(ntri, S_t[:])]
            rd = [S_b, cstb]
            if not r["first"]:
                terms.append((ones1, cr_t[:]))
                rd.append(cr_b)
            mm(r["bb"], psum[r["bb"]][:], terms, rd)

        def s3(t):
            r = res(t)
            bb = r["bb"]
            cr_t, cr_b = r["cr"]
            E_t, E_b = r["E"]
            X_t, X_b = r["X"]
            a_t, a_b = r["a"]
            if not r["last"]:
                dve(lambda e: e.tensor_copy(cr_t[:], psum[bb][0:1, :]), [psb[bb]], [cr_b])
            act(lambda e: e.activation(X_t[:], psum[bb][:], AF.Exp), [psb[bb]], [X_b])
            dve(lambda e: e.tensor_tensor(a_t[:], E_t[:], X_t[:], ALU.mult), [E_b, X_b], [a_b])

        def s4(t):
            r = res(t)
            a_t, a_b = r["a"]
            ob, hs, p, j = r["ob"], r["hs"], r["p"], r["j"]
            mm(ob, psum[ob][:], [(vsb[:, j, 128 * p:128 * p + 128], a_t[:])], [vsbb[j], a_b],
               start=r["first"], stop=r["last"])
            if r["last"]:
                dve(lambda e: e.tensor_copy(sbT[hs, p, :], psum[ob][hs, :]), [psb[ob]], [sbTb])

        for st in range(N + 4):
            if 0 <= st - 3 < N:
                s3(tiles[st - 3])
            if 0 <= st - 2 < N:
                s2(tiles[st - 2])
            if 0 <= st - 4 < N:
                s4(tiles[st - 4])
            if 0 <= st - 1 < N:
                s1(tiles[st - 1])
            if st < N:
                s0(tiles[st])
            for _ in range(rate):
                if fpos[0] < len(filler):
                    filler[fpos[0]]()
                    fpos[0] += 1
        while fpos[0] < len(filler):
            filler[fpos[0]]()
            fpos[0] += 1
        for ci in range(4):
            cs = slice(ci * 128, (ci + 1) * 128)
            for half in range(2):
                bk = pbank()
                mm(bk, psum[bk][:], [(sbT[:, p, cs], wsob[:, p, half * 512:(half + 1) * 512]) for p in range(2)],
                   [sbTb, wsob_b])
                dve(lambda e, bk=bk, half=half: e.tensor_copy(ys[:, half * 512:(half + 1) * 512], psum[bk][:]),
                    [psb[bk]], [ysb])
            store(SP, rsin[k][cs, D:2 * D], ys[:], ysb, rsinb[k])

    for _ in phase1_proj(0):
        pass
    for _ in phase1_proj(1):
        pass
    def deferred_proj(k):
        DEFER[0] = []
        for _ in phase1_proj(k):
            pass
        th = DEFER[0]
        DEFER[0] = None
        return th

    for k in range(1, nsc + 1):
        if True:
            phase1_sb(k, deferred_proj(k + 1) if k < nsc else None)
            S.dma(POOL, (lambda e, k=k: e.collective_compute("ReduceScatter", ALU.add,
                                                             replica_groups=[[0, 1, 2, 3], [4, 5, 6, 7]],
                                                             ins=[rsin[k].opt()], outs=[rsout[k].opt()])),
                  [rsinb[k]], [rsoutb[k]], rsoutb[k], inc=1)

    def barrier_all():
        evs = [(e_, S.cnt[e_]) for e_ in (PE, ACT, DVE, POOL)] + [(b_.sem, b_.inc * b_.cnt) for b_ in S.owners]
        for eng in ENGS:
            out_ = []
            for k_, v_ in evs:
                if v_ > 0 and S.seen[eng].get(k_, 0) < v_ and not (k_ == eng and eng == PE):
                    S.seen[eng][k_] = v_
                    out_.append((k_, v_))
            S.ops[eng].append((out_, None, None))

    barrier_all()
    A.off = p1_mark
    wgb, wgb_b = T("wgb", [128, 8, 2048], BF16)
    woutb, woutb_b = T("woutb", [128, 8, D], BF16)
    gp_t, gp_b = T("gpost", [128, 2 * D])
    load(SP, gp_t[:], gpost, gp_b)
    load(SP, wgb[:], wg_d.rearrange("(kt p) c -> p kt c", p=128), wgb_b, reads=[precast_bufs["wg"]])
    load(SP, woutb[:], wout_d.rearrange("(kt p) c -> p kt c", p=128), woutb_b, reads=[precast_bufs["wout"]])
    hnT2, hnT2b = T("hnT2", [128, 8, 128], BF16)
    x2 = [T("x2_%d" % i, [128, D]) for i in range(2)]
    yrs = [T("yrs%d" % i, [128, 2 * D]) for i in range(2)]
    sg, sgb = T("sg", [128, 2 * D])
    mg, mgb = T("mg", [128, D])
    mgh, mghb = T("mgh", [128, D], BF16)
    mgT, mgTb = T("mgT", [128, 8, 128], BF16)
    h1, h1b = T("h1", [128, D])
    h1db = [Buf("h1d%d" % i) for i in range(NT)]

    for t in range(NT):
        k = t + 1
        x_t, x_b = x2[t % 2]
        y_t, y_b = yrs[t % 2]
        load(SP, x_t[:], xown[t * 128:(t + 1) * 128, :], x_b)
        load(SP, y_t[:], rsout[k], y_b, reads=[rsoutb[k]])
        norm_and_transpose(x_t, x_b, hnT2, hnT2b, 0)
        for q4 in range(4):
            bk = pbank()
            mm(bk, psum[bk][:], [(hnT2[:, kt, :], wgb[:, kt, q4 * 512:(q4 + 1) * 512]) for kt in range(8)],
               [hnT2b, wgb_b])
            act(lambda e, bk=bk, q4=q4: e.activation(sg[:, q4 * 512:(q4 + 1) * 512], psum[bk][:], AF.Exp, scale=-1.0),
                [psb[bk]], [sgb])
        act(lambda e: e.activation(sg[:], sg[:], AF.Ln, bias=1.0), [sgb], [sgb])
        act(lambda e: e.activation(sg[:], sg[:], AF.Exp, scale=-1.0), [sgb], [sgb])
        dve(lambda e, y_t=y_t: e.tensor_tensor(sg[:], sg[:], y_t[:], ALU.mult), [sgb, y_b], [sgb])
        dve(lambda e: e.tensor_tensor(mgh[:], sg[:, 0:D], sg[:, D:2 * D], ALU.add), [sgb], [mghb])
        transpose_to(mgh, mghb, 8, mgT, mgTb, 0)
        for half in range(2):
            bk = pbank()
            mm(bk, psum[bk][:], [(mgT[:, kt, :], woutb[:, kt, half * 512:(half + 1) * 512]) for kt in range(8)],
               [mgTb, woutb_b])
            dve(lambda e, bk=bk, half=half: e.tensor_copy(mg[:, half * 512:(half + 1) * 512], psum[bk][:]),
                [psb[bk]], [mgb])
        rms_tile(mg, mgb)
        dve(lambda e: e.tensor_scalar(mg[:], mg[:], ss[:, 1:2], None, ALU.mult), [mgb, ssb], [mgb])
        dve(lambda e: e.tensor_tensor(mg[:], mg[:], gp_t[:, 0:D], ALU.mult), [mgb, gp_b], [mgb])
        dve(lambda e, x_t=x_t: e.tensor_tensor(h1[:], mg[:], x_t[:], ALU.add), [mgb, x_b], [h1b])
        store(SP, h1d[t * 128:(t + 1) * 128, :], h1[:], h1b, h1db[t])

    barrier_all()
    A.off = p1_mark
    gp2_t, gp2_b = T("gpost2", [128, D])
    load(SP, gp2_t[:], gpost[:, D:2 * D], gp2_b)
    wfib, wfib_b = T("wfib", [128, 8, 2 * DFF], BF16)
    wfob, wfob_b = T("wfob", [128, NFT, D], BF16)
    for kt_ in range(8):
        load(SP, wfib[:, kt_, :], wfi_d[kt_ * 128:(kt_ + 1) * 128, :], wfib_b, reads=[precast_bufs["wfi"]])
    for f_ in range(0, NFT, 2):
        load(SP, wfob[:, f_:f_ + 2, :], wfo_d[f_ * 128:(f_ + 2) * 128, :].rearrange("(f p) c -> p f c", p=128), wfob_b,
             reads=[precast_bufs["wfo"]])
    hb = [T("hb%d" % i, [128, D]) for i in range(2)]
    hnT3, hnT3b = T("hnT3", [128, 8, 128], BF16)
    uT, uTb = T("uT", [128, NFT, 128], BF16)
    ea, eab = T("ea", [128, 512])
    ff, ffb = T("ff", [128, D])
    ot = [T("ot%d" % i, [128, D]) for i in range(2)]
    outb = [Buf("out%d" % i) for i in range(NT)]

    for t in range(NT):
        h_t, h_b = hb[t % 2]
        o_t, o_b = ot[t % 2]
        load(SP, h_t[:], h1d[t * 128:(t + 1) * 128, :], h_b, reads=[h1db[t]])
        norm_and_transpose(h_t, h_b, hnT3, hnT3b, 0)
        for f0 in range(0, NFT, 4):
            fn_ = min(4, NFT - f0)
            ba = pbank()
            bbk = 4 + (f0 // 4) % 2
            for ff_ in range(fn_):
                f = f0 + ff_
                mm(ba, psum[ba][:, ff_ * 128:(ff_ + 1) * 128],
                   [(wfib[:, kt, f * 128:(f + 1) * 128], hnT3[:, kt, :]) for kt in range(8)], [wfib_b, hnT3b])
            for ff_ in range(fn_):
                f = f0 + ff_
                mm(bbk, psum[bbk][:, ff_ * 128:(ff_ + 1) * 128],
                   [(wfib[:, kt, DFF + f * 128:DFF + (f + 1) * 128], hnT3[:, kt, :]) for kt in range(8)],
                   [wfib_b, hnT3b])
            w_ = fn_ * 128
            act(lambda e, ba=ba, w_=w_: e.activation(ea[:, :w_], psum[ba][:, :w_], AF.Exp, scale=-1.0),
                [psb[ba]], [eab])
            act(lambda e, w_=w_: e.activation(ea[:, :w_], ea[:, :w_], AF.Ln, bias=1.0), [eab], [eab])
            act(lambda e, w_=w_: e.activation(ea[:, :w_], ea[:, :w_], AF.Exp, scale=-1.0), [eab], [eab])
            dve(lambda e, ba=ba, w_=w_: e.tensor_tensor(ea[:, :w_], ea[:, :w_], psum[ba][:, :w_], ALU.mult),
                [eab, psb[ba]], [eab])
            dve(lambda e, bbk=bbk, w_=w_, f0=f0, fn_=fn_: e.tensor_tensor(
                uT[:, f0:f0 + fn_, :], ea[:, :w_].rearrange("p (f t) -> p f t", t=128),
                psum[bbk][:, :w_].rearrange("p (f t) -> p f t", t=128), ALU.mult),
                [eab, psb[bbk]], [uTb])
        for half in range(2):
            bk = pbank()
            mm(bk, psum[bk][:], [(uT[:, f, :], wfob[:, f, half * 512:(half + 1) * 512]) for f in range(NFT)],
               [uTb, wfob_b])
            dve(lambda e, bk=bk, half=half: e.tensor_copy(ff[:, half * 512:(half + 1) * 512], psum[bk][:]),
                [psb[bk]], [ffb])
        rms_tile(ff, ffb)
        dve(lambda e: e.tensor_scalar(ff[:], ff[:], ss[:, 1:2], None, ALU.mult), [ffb, ssb], [ffb])
        dve(lambda e: e.tensor_tensor(ff[:], ff[:], gp2_t[:], ALU.mult), [ffb, gp2_b], [ffb])
        dve(lambda e, o_t=o_t, h_t=h_t: e.tensor_tensor(o_t[:], ff[:], h_t[:], ALU.add), [ffb, h_b], [o_b])
        store(SP, out[t * 128:(t + 1) * 128, :], o_t[:], o_b, outb[t])
    S.final_waits(SP, outb)

    return finish()


_NC_CACHE = {}


def _consts():
    ident = np.eye(128, dtype=np.float32)
    s = np.arange(128)
    ntri = -(s[:, None] >= s[None, :]).astype(np.float32)
    ones = np.ones((128, 128), np.float32)
    rmaskT = (s[:, None] <= s[None, :]).astype(np.float32)
    t = np.arange(512)
    masks = []
    for d in range(4):
        masks.append(((128 * d + s[:, None]) < t[None, :]).astype(np.float32))
    masks.append(np.broadcast_to((s[:, None] >= PAD), (128, 512)).astype(np.float32))
    masks.append(np.zeros((128, 512), np.float32))
    return np.concatenate([ident, ntri, ones, rmaskT] + masks, axis=1).astype(ml_dtypes.bfloat16)


def kernel(x, meta_tokens, w_in, w_ret_out, w_sb_out, w_out, w_ffn_in, w_ffn_out,
           norm_mix_pre, norm_mix_post, norm_ffn_pre, norm_ffn_post):
    x = np.asarray(x, np.float32)
    B, SEQ, _ = x.shape
    nsc = SEQ // 512
    nch = 1 + 4 * nsc
    LP = 128 * nch
    if nsc not in _NC_CACHE:
        _NC_CACHE[nsc] = build(nsc)
    nc = _NC_CACHE[nsc]
    w_in = np.asarray(w_in, np.float32)[0]
    cb = _consts()
    gam = np.concatenate([np.asarray(norm_mix_pre, np.float32)[0].reshape(8, 128).T,
                          np.asarray(norm_ffn_pre, np.float32)[0].reshape(8, 128).T], axis=1)
    gpost = np.concatenate([np.broadcast_to(np.asarray(norm_mix_post, np.float32)[0], (128, D)),
                            np.broadcast_to(np.asarray(norm_ffn_post, np.float32)[0], (128, D))], axis=1)
    pos = (np.arange(LP, dtype=np.float32) - np.float32(PAD))
    inv = (np.float32(10000.0) ** (-np.arange(128, dtype=np.float32) / np.float32(128))).astype(np.float32)
    ang = (inv[:, None] * pos[None, :]).astype(np.float32)
    cosv, sinv = np.cos(ang).astype(np.float32), np.sin(ang).astype(np.float32)
    loc = (np.arange(LP) % 128).astype(np.float64)
    in_maps = []
    for c in range(8):
        b, g = c // 4, c % 4
        log_g = np.log1p(-(2.0 ** (-5.0 - g)))
        qsc = (np.exp(log_g * (loc + 1.0)) * (256.0 ** -0.5)).astype(np.float32)
        ksc = np.exp(-log_g * (loc + 1.0)).astype(np.float32)
        rtab = np.stack([cosv * qsc, sinv * qsc, cosv * ksc, sinv * ksc]).astype(np.float32)
        gchv = np.full((128, 1), np.exp(log_g * 128.0), np.float32)
        xall = np.concatenate([np.zeros((PAD, D), np.float32), np.asarray(meta_tokens, np.float32), x[b]], axis=0)
        xown = x[b].reshape(nsc, 4, 128, D)[:, g].reshape(nsc * 128, D)
        cols = np.concatenate([
            np.arange(256 * g, 256 * g + 256),
            1024 + np.arange(256 * g, 256 * g + 256),
            2048 + np.arange(512 * g, 512 * g + 512),
            4096 + np.arange(512 * g, 512 * g + 512),
            6144 + np.arange(256 * g, 256 * g + 256),
            7168 + np.arange(256 * g, 256 * g + 256),
            8192 + np.arange(256 * g, 256 * g + 256),
        ])
        in_maps.append({
            "xall": np.ascontiguousarray(xall),
            "xown": np.ascontiguousarray(xown),
            "w1": np.ascontiguousarray(w_in[:, cols]),
            "wg": np.ascontiguousarray(w_in[:, 9216:11264]),
            "wro": np.ascontiguousarray(np.asarray(w_ret_out, np.float32)[0][512 * g:512 * g + 512]),
            "wso": np.ascontiguousarray(np.asarray(w_sb_out, np.float32)[0][256 * g:256 * g + 256]),
            "wout": np.ascontiguousarray(np.asarray(w_out, np.float32)[0]),
            "wfi": np.ascontiguousarray(np.asarray(w_ffn_in, np.float32)[0]),
            "wfo": np.ascontiguousarray(np.asarray(w_ffn_out, np.float32)[0]),
            "gam": np.ascontiguousarray(gam),
            "gpost": np.ascontiguousarray(gpost),
            "rtab": np.ascontiguousarray(rtab),
            "gch": gchv,
            "cbf": cb,
        })
    res = run_bass_kernel_spmd(nc, in_maps, core_ids=list(range(8)))
    outp = np.zeros((B, SEQ, D), np.float32)
    for c in range(8):
        b, g = c // 4, c % 4
        o = np.asarray(res.results[c]["out"], np.float32).reshape(nsc, 128, D)
        outp[b].reshape(nsc, 4, 128, D)[:, g] = o
    return outp
```

```python
from contextlib import ExitStack
import numpy as np
import ml_dtypes
import concourse.bass as bass
import concourse.mybir as mybir
from concourse.bass_utils import run_bass_kernel_spmd

F32 = mybir.dt.float32
BF16 = mybir.dt.bfloat16
AF = mybir.ActivationFunctionType
ALU = mybir.AluOpType

D = 1024
NMETA = 16
PAD = 112
DFF = 2816
NFT = DFF // 128
PE, ACT, DVE, POOL, SP = "pe", "act", "dve", "pool", "sp"
ENGS = [PE, ACT, DVE, POOL, SP]
SBUF_LO = 16512
SBUF_HI = 229376
import os
NO_CC = bool(int(os.environ.get('NO_CC', '0')))


class Buf:
    __slots__ = ("name", "w", "r", "sem", "cnt", "excl", "inc")

    def __init__(self, name, excl=False):
        self.name = name
        self.excl = excl
        self.w = None
        self.r = {}
        self.sem = None
        self.cnt = 0


class Sched:
    def __init__(self):
        self.ops = {e: [] for e in ENGS}
        self.cnt = {e: 0 for e in ENGS}
        self.seen = {e: {} for e in ENGS}
        self.dsems = []
        self.owners = []

    def _waits(self, eng, reads, writes):
        ev = []
        for b in reads:
            if b.w is not None:
                ev.append(b.w)
        for b in writes:
            if b.w is not None:
                ev.append(b.w)
            ev.extend(b.r.items())
        out = {}
        for k, v in ev:
            if k == eng and eng == PE:
                continue
            if self.seen[eng].get(k, 0) >= v:
                continue
            out[k] = max(out.get(k, 0), v)
        for k, v in out.items():
            self.seen[eng][k] = v
        return list(out.items())

    def op(self, eng, fn, reads=(), writes=(), signal=True):
        writes = list(writes) + [b for b in reads if b.excl and b not in writes]
        waits = self._waits(eng, reads, writes)
        if signal:
            self.cnt[eng] += 1
            v = self.cnt[eng]
        else:
            v = self.cnt[eng] + 1
        for b in reads:
            b.r[eng] = max(b.r.get(eng, 0), v)
        for b in writes:
            b.w = (eng, v)
            b.r = {}
        self.ops[eng].append((waits, fn, (eng, 1) if signal else None))

    def dma(self, q, fn, reads, writes, owner, inc=16):
        waits = self._waits(q, reads, writes)
        if owner.sem is None:
            owner.sem = "d%d" % len(self.dsems)
            self.dsems.append(owner.sem)
            self.owners.append(owner)
        owner.cnt += 1
        ev = (owner.sem, inc * owner.cnt)
        for b in reads:
            b.r[ev[0]] = max(b.r.get(ev[0], 0), ev[1])
        for b in writes:
            b.w = ev
            b.r = {}
        self.ops[q].append((waits, fn, (owner.sem, inc)))
        owner.inc = inc

    def final_waits(self, eng, bufs):
        waits = self._waits(eng, bufs, ())
        self.ops[eng].append((waits, None, None))


def build(nsc):
    phase = 1
    nch = 1 + 4 * nsc
    LP = 128 * nch
    NT = nsc
    nc = bass.Bass("TRN2", target_bir_lowering=False)
    S = Sched()

    def din(name, shape, dt=F32):
        return nc.dram_tensor(name, list(shape), dt, kind="ExternalInput").ap()

    xall = din("xall", [LP, D])
    xown = din("xown", [NT * 128, D])
    w1 = din("w1", [D, 2304])
    wg = din("wg", [D, 2048])
    wro = din("wro", [512, D])
    wso = din("wso", [256, D])
    wout = din("wout", [D, D])
    wfi = din("wfi", [D, 2 * DFF])
    wfo = din("wfo", [DFF, D])
    gam = din("gam", [128, 16])
    gpost = din("gpost", [128, 2 * D])
    rtab = din("rtab", [4, 128, LP])
    gch = din("gch", [128, 1])
    cbf = din("cbf", [128, 128 * 4 + 512 * 6], BF16)
    rsin = [nc.dram_tensor("rsin%d" % k, [512, 2048], F32).ap() for k in range(nsc + 1)]
    rsout = [nc.dram_tensor("rsout%d" % k, [128, 2048], F32).ap() for k in range(nsc + 1)]
    out = nc.dram_tensor("out", [NT * 128, D], F32, kind="ExternalOutput").ap()
    h1d = nc.dram_tensor("h1d", [NT * 128, D], F32).ap()
    wg_d = nc.dram_tensor("wg_d", [D, 2048], BF16).ap()
    wout_d = nc.dram_tensor("wout_d", [D, D], BF16).ap()
    wfi_d = nc.dram_tensor("wfi_d", [D, 2 * DFF], BF16).ap()
    wfo_d = nc.dram_tensor("wfo_d", [DFF, D], BF16).ap()

    class Arena:
        def __init__(self):
            self.off = SBUF_LO
            self.n = 0

        def alloc(self, name, shape, dt):
            nb = int(np.prod(shape[1:])) * (4 if dt == F32 else 2)
            nb = (nb + 31) // 32 * 32
            assert self.off + nb <= SBUF_HI, ("SBUF overflow", name, self.off, nb)
            h = nc.alloc_sbuf_tensor_at("%s_%d" % (name, self.n), list(shape), dt, offset=self.off)
            self.n += 1
            self.off += nb
            return h

    A = Arena()

    def T(name, shape, dt=F32):
        return A.alloc(name, shape, dt), Buf(name)

    psum = [nc.alloc_psum_tensor("ps%d" % i, [128, 512], F32) for i in range(8)]
    psb = [Buf("ps%d" % i, excl=True) for i in range(8)]

    DEFER = [None]

    def emit(thunk):
        if DEFER[0] is not None:
            DEFER[0].append(thunk)
        else:
            thunk()

    def mm(bank, out_ap, terms, reads, start=True, stop=True):
        reads = list(reads)

        def go():
            n = len(terms)
            for i, (l, r) in enumerate(terms):
                st = start and i == 0
                sp = stop and i == n - 1
                S.op(PE, (lambda e, l=l, r=r, st=st, sp=sp: e.matmul(out_ap, l, r, start=st, stop=sp)),
                     reads=reads, writes=[psb[bank]], signal=(i == n - 1))
        emit(go)

    def dve(fn, reads, writes):
        reads, writes = list(reads), list(writes)
        emit(lambda: S.op(DVE, fn, reads, writes))

    def act(fn, reads, writes):
        reads, writes = list(reads), list(writes)
        emit(lambda: S.op(ACT, fn, reads, writes))

    def pool(fn, reads, writes):
        reads, writes = list(reads), list(writes)
        emit(lambda: S.op(POOL, fn, reads, writes))

    def load(q, dst_ap, src_ap, dstbuf, reads=()):
        reads = list(reads)
        emit(lambda: S.dma(q, (lambda e: e.dma_start(out=dst_ap, in_=src_ap)), reads, [dstbuf], dstbuf))

    def store(q, dst_ap, src_ap, srcbuf, dstbuf):
        emit(lambda: S.dma(q, (lambda e: e.dma_start(out=dst_ap, in_=src_ap)), [srcbuf], [dstbuf], srcbuf))

    cst, cstb = T("cst", [128, 128 * 4 + 512 * 6], BF16)
    load(SP, cst[:], cbf, cstb)
    ident = cst[:, 0:128]
    ntri = cst[:, 128:256]
    ones1 = cst[0:1, 256:384]
    rmaskT = cst[:, 384:512]

    def sbmask(i, Tn):
        return cst[:, 512 + 512 * i: 512 + 512 * i + Tn]

    gam_t, gam_b = T("gam", [128, 16])
    load(SP, gam_t[:], gam, gam_b)
    gch_t, gch_b = T("gch", [128, 1])
    load(SP, gch_t[:], gch, gch_b)

    stg = [T("stg%d" % i, [128, 512]) for i in range(3)]
    stg_i = [0]

    def load_weight(dst, dstbuf, src, nkt, ncols, gcol=None):
        for kt in range(nkt):
            for c0 in range(0, ncols, 512):
                cw = min(512, ncols - c0)
                st, stb = stg[stg_i[0] % 3]
                stg_i[0] += 1
                load(SP, st[:, :cw], src[kt * 128:(kt + 1) * 128, c0:c0 + cw], stb)
                o = dst[:, kt, c0:c0 + cw]
                if gcol is None:
                    dve((lambda e, o=o, st=st, cw=cw: e.tensor_copy(o, st[:, :cw])), [stb], [dstbuf])
                else:
                    g = gam_t[:, gcol + kt:gcol + kt + 1]
                    dve((lambda e, o=o, st=st, cw=cw, g=g: e.tensor_scalar(o, st[:, :cw], g, None, ALU.mult)),
                        [stb, gam_b], [dstbuf])

    def finish():
        with ExitStack() as es:
            sems = {}
            for e_ in ENGS:
                sems[e_] = es.enter_context(nc.semaphore("c_" + e_))
            for dname in S.dsems:
                sems[dname] = es.enter_context(nc.semaphore(dname))
            block = es.enter_context(nc.Block())

            def replay(name, eng):
                for waits, fn, inc in S.ops[name]:
                    for k_, v_ in waits:
                        eng.wait_ge(sems[k_], v_)
                    if fn is None:
                        continue
                    ins = fn(eng)
                    if inc is not None:
                        ins.then_inc(sems[inc[0]], inc[1])

            @block.tensor
            def _(e):
                replay(PE, e)

            @block.scalar
            def _(e):
                replay(ACT, e)

            @block.vector
            def _(e):
                replay(DVE, e)

            @block.gpsimd
            def _(e):
                replay(POOL, e)

            @block.sync
            def _(e):
                replay(SP, e)

        return nc

    hn, hnb = T("hn", [128, D], BF16)
    ss, ssb = T("ss", [128, 2])
    p1_mark = A.off
    wbf, wbf_b = T("wbf", [128, 8, 2304], BF16)
    wrob, wrob_b = T("wrob", [128, 4, D], BF16)
    wsob, wsob_b = T("wsob", [128, 2, D], BF16)
    if phase == 1:
        load_weight(wbf, wbf_b, w1, 8, 2304, gcol=0)
        load_weight(wrob, wrob_b, wro, 4, D)
        load_weight(wsob, wsob_b, wso, 2, D)
    RQ, RK, RV, RG, SQ, SK, SV = 0, 256, 512, 1024, 1536, 1792, 2048

    kT, _ = T("kT", [128, 2, LP], BF16)
    kTb = [[Buf("kT%d_%d" % (p, c)) for c in range(nch)] for p in range(2)]
    vsb, _ = T("vsb", [128, nch, 256], BF16)
    vsbb = [Buf("vsb%d" % c) for c in range(nch)]
    hnT, hnTb = T("hnT", [128, 8, 512], BF16)
    xt = [T("xt%d" % i, [128, D]) for i in range(2)]
    tab, tabb = T("tab", [128, 4, 512])
    qTr, qTrb = T("qTr", [128, 2, 512], BF16)
    kTr, kTrb = T("kTr", [128, 2, 512], BF16)
    qs = [T("qs%d" % i, [128, 2, 512], BF16) for i in range(2)]
    tmp = [T("tmp%d" % i, [128, 512]) for i in range(2)]
    vr = [T("vr%d" % i, [128, 512], BF16) for i in range(2)]
    rgs = [T("rgs%d" % i, [128, 512]) for i in range(2)]
    ktok, ktokb = T("ktok", [128, 256], BF16)
    scm, scmb = T("scm", [128, 128], BF16)
    Rst, Rstb = T("Rst", [128, 2, 512])
    stbf, stbfb = T("stbf", [128, 2, 512], BF16)
    stats, statsb = T("stats", [128, 8])
    gated, gatedb = T("gated", [128, 512], BF16)
    gT, gTb = T("gT", [128, 4, 128], BF16)
    sbT, sbTb = T("sbT", [128, 2, 512], BF16)
    ys, ysb = T("ys", [128, D])
    NE = 3
    Eb = [stg[i] for i in range(3)]
    Sb = [T("S%d" % i, [128, 512], BF16) for i in range(NE)]
    Xb = [T("X%d" % i, [128, 512]) for i in range(2)]
    ab = [T("a%d" % i, [128, 512], BF16) for i in range(NE)]
    crow = [T("crow%d" % i, [1, 512], BF16) for i in range(2)]

    pst = [T("pst%d" % i, [128, 512]) for i in range(2)]
    pob = [T("pob%d" % i, [128, 512], BF16) for i in range(2)]
    precast_bufs = {}

    def precast(name, dst, src, nkt, ncols, gcol=None):
        db = Buf("pc_" + name)
        precast_bufs[name] = db
        pieces = [(kt, c0, min(512, ncols - c0)) for kt in range(nkt) for c0 in range(0, ncols, 512)]

        def L(i):
            kt, c0, cw = pieces[i]
            st, stb = pst[i % 2]
            load(POOL, st[:, :cw], src[kt * 128:(kt + 1) * 128, c0:c0 + cw], stb)

        def C(i):
            kt, c0, cw = pieces[i]
            st, stb = pst[i % 2]
            ob_, obb = pob[i % 2]
            if gcol is None:
                pool((lambda e: e.tensor_copy(ob_[:, :cw], st[:, :cw])), [stb], [obb])
            else:
                g = gam_t[:, gcol + kt:gcol + kt + 1]
                pool((lambda e: e.tensor_scalar(ob_[:, :cw], st[:, :cw], g, None, ALU.mult)), [stb, gam_b], [obb])
            store(POOL, dst[kt * 128:(kt + 1) * 128, c0:c0 + cw], ob_[:, :cw], obb, db)

        L(0)
        for i in range(len(pieces)):
            if i + 1 < len(pieces):
                L(i + 1)
            C(i)

    precast("wg", wg_d, wg, 8, 2048, gcol=0)
    precast("wout", wout_d, wout, 8, D)
    precast("wfi", wfi_d, wfi, 8, 2 * DFF, gcol=8)
    precast("wfo", wfo_d, wfo, NFT, D)

    dve(lambda e: e.memset(Rst[:], 0.0), [], [Rstb])
    dve(lambda e: e.memset(stbf[:], 0.0), [], [stbfb])

    pctr = [0]

    def pbank():
        pctr[0] += 1
        return 6 + (pctr[0] % 2)

    CTX0 = {}

    def rms_tile(xtile, xbuf, ctx=None):
        c = ctx or CTX0
        hn_, hnb_ = c["hn"]
        ss_, ssb_ = c["ss"]
        dve(lambda e: e.memset(ss_[:, 0:1], 0.0), [], [ssb_])
        act(lambda e: e.activation(hn_[:], xtile[:], AF.Square, accum_out=ss_[:, 0:1]), [xbuf, ssb_], [hnb_, ssb_])
        act(lambda e: e.activation(ss_[:, 1:2], ss_[:, 0:1], AF.Ln, bias=1e-6, scale=1.0 / D), [ssb_], [ssb_])
        act(lambda e: e.activation(ss_[:, 1:2], ss_[:, 1:2], AF.Exp, scale=-0.5), [ssb_], [ssb_])

    def norm_and_transpose(xtile, xbuf, dstT, dstTb, col0, ctx=None):
        c = ctx or CTX0
        hn_, hnb_ = c["hn"]
        ss_, ssb_ = c["ss"]
        rms_tile(xtile, xbuf, c)
        dve(lambda e: e.tensor_scalar(hn_[:], xtile[:], ss_[:, 1:2], None, ALU.mult), [xbuf, ssb_], [hnb_])
        transpose_to(hn_, hnb_, 8, dstT, dstTb, col0, c)

    def transpose_to(src, srcb, nkt, dstT, dstTb, col0, ctx=None):
        c = ctx or CTX0
        for k0 in range(0, nkt, 4):
            kn = min(4, nkt - k0)
            bk = c["bank"]()
            for kk in range(kn):
                kt = k0 + kk
                mm(bk, psum[bk][:, kk * 128:(kk + 1) * 128], [(src[:, kt * 128:(kt + 1) * 128], ident)],
                   [srcb, cstb])
            for kk in range(kn):
                kt = k0 + kk
                dve(lambda e, bk=bk, kk=kk, kt=kt: e.tensor_copy(dstT[:, kt, col0:col0 + 128],
                                                                 psum[bk][:, kk * 128:(kk + 1) * 128]),
                    [psb[bk]], [dstTb])

    CTX0.update(hn=(hn, hnb), ss=(ss, ssb), bank=pbank)

    def phase1_proj(k):
        chunks = [0] if k == 0 else list(range(4 * k - 3, 4 * k + 1))
        Tn = 128 * len(chunks)
        tok0 = 128 * chunks[0]
        load(SP, tab[:, :, :Tn], rtab[:, :, tok0:tok0 + Tn].rearrange("a p t -> p a t"), tabb)
        for ci, c in enumerate(chunks):
            x_t, x_b = xt[c % 2]
            load(SP, x_t[:], xall[c * 128:(c + 1) * 128, :], x_b)
            norm_and_transpose(x_t, x_b, hnT, hnTb, ci * 128)
            yield

        def fm_proj(col0):
            bk = pbank()
            mm(bk, psum[bk][:, :Tn], [(wbf[:, kt, col0:col0 + 128], hnT[:, kt, :Tn]) for kt in range(8)],
               [wbf_b, hnTb])
            return bk

        for (col, dst, dstb, tc, ts) in ((RQ, qTr, qTrb, 0, 1), (RK, kTr, kTrb, 2, 3)):
            b1 = fm_proj(col)
            b2 = fm_proj(col + 128)
            t0, t0b = tmp[0]
            t1, t1b = tmp[1]
            dve(lambda e, b1=b1, tc=tc: e.tensor_tensor(t0[:, :Tn], psum[b1][:, :Tn], tab[:, tc, :Tn], ALU.mult),
                [psb[b1], tabb], [t0b])
            dve(lambda e, b2=b2, ts=ts: e.tensor_tensor(t1[:, :Tn], psum[b2][:, :Tn], tab[:, ts, :Tn], ALU.mult),
                [psb[b2], tabb], [t1b])
            dve(lambda e, dst=dst: e.tensor_tensor(dst[:, 0, :Tn], t0[:, :Tn], t1[:, :Tn], ALU.subtract),
                [t0b, t1b], [dstb])
            yield
            dve(lambda e, b1=b1, ts=ts: e.tensor_tensor(t0[:, :Tn], psum[b1][:, :Tn], tab[:, ts, :Tn], ALU.mult),
                [psb[b1], tabb], [t0b])
            dve(lambda e, b2=b2, tc=tc: e.tensor_tensor(t1[:, :Tn], psum[b2][:, :Tn], tab[:, tc, :Tn], ALU.mult),
                [psb[b2], tabb], [t1b])
            dve(lambda e, dst=dst: e.tensor_tensor(dst[:, 1, :Tn], t0[:, :Tn], t1[:, :Tn], ALU.add),
                [t0b, t1b], [dstb])
            yield
        q_t, q_b = qs[k % 2]
        for p in range(2):
            bk = fm_proj(SQ + 128 * p)
            dve(lambda e, bk=bk, p=p: e.tensor_scalar(q_t[:, p, :Tn], psum[bk][:, :Tn], 0.125, None, ALU.mult),
                [psb[bk]], [q_b])
            bk = fm_proj(SK + 128 * p)
            dve(lambda e, bk=bk, p=p: e.tensor_copy(kT[:, p, tok0:tok0 + Tn], psum[bk][:, :Tn]),
                [psb[bk]], [kTb[p][c] for c in chunks])
            yield
        for ci, c in enumerate(chunks):
            cs = slice(ci * 128, (ci + 1) * 128)

            def tm_proj(col0, ncols):
                bk = pbank()
                mm(bk, psum[bk][:, :ncols], [(hnT[:, kt, cs], wbf[:, kt, col0:col0 + ncols]) for kt in range(8)],
                   [wbf_b, hnTb])
                return bk

            v_t, v_b = vr[c % 2]
            bk = tm_proj(RV, 512)
            dve(lambda e, bk=bk, v_t=v_t: e.tensor_copy(v_t[:], psum[bk][:]), [psb[bk]], [v_b])
            yield
            bk = tm_proj(SV, 256)
            dve(lambda e, bk=bk, c=c: e.tensor_copy(vsb[:, c, :], psum[bk][:, :256]), [psb[bk]], [vsbb[c]])
            yield
            rg_t, rg_b = rgs[c % 2]
            if k > 0:
                bk = tm_proj(RG, 512)
                dve(lambda e, bk=bk, rg_t=rg_t: e.tensor_copy(rg_t[:], psum[bk][:]), [psb[bk]], [rg_b])
            bk = pbank()
            for kt in range(2):
                mm(bk, psum[bk][:, kt * 128:(kt + 1) * 128], [(kTr[:, kt, cs], ident)], [kTrb, cstb])
            dve(lambda e, bk=bk: e.tensor_copy(ktok[:], psum[bk][:, :256]), [psb[bk]], [ktokb])
            yield
            if k > 0:
                bk = pbank()
                mm(bk, psum[bk][:, :128], [(kTr[:, kt, cs], qTr[:, kt, cs]) for kt in range(2)], [kTrb, qTrb])
                dve(lambda e, bk=bk: e.tensor_tensor(scm[:], psum[bk][:, :128], rmaskT, ALU.mult),
                    [psb[bk], cstb], [scmb])
                bo = pbank()
                mm(bo, psum[bo][:], [(scm[:], v_t[:])] + [(qTr[:, kt, cs], stbf[:, kt, :]) for kt in range(2)],
                   [scmb, v_b, qTrb, stbfb])
            if k == 0:
                for kt in range(2):
                    bk = pbank()
                    mm(bk, psum[bk][:], [(ktok[:, kt * 128:(kt + 1) * 128], v_t[:])], [ktokb, v_b])
                    dve(lambda e, bk=bk, kt=kt: e.scalar_tensor_tensor(Rst[:, kt, :], Rst[:, kt, :], gch_t[:, 0:1],
                                                                        psum[bk][:], ALU.mult, ALU.add),
                        [Rstb, gch_b, psb[bk]], [Rstb])
                dve(lambda e: e.tensor_scalar(stbf[:], Rst[:], gch_t[:, 0:1], None, ALU.mult), [Rstb, gch_b], [stbfb])
                continue
            dve(lambda e, bo=bo: e.bn_stats(stats[:, 0:6], psum[bo][:]), [psb[bo]], [statsb])
            dve(lambda e: e.bn_aggr(stats[:, 6:8], stats[:, 0:6]), [statsb], [statsb])
            act(lambda e: e.activation(stats[:, 7:8], stats[:, 7:8], AF.Ln, bias=1e-5), [statsb], [statsb])
            act(lambda e: e.activation(stats[:, 7:8], stats[:, 7:8], AF.Exp, scale=-0.5), [statsb], [statsb])
            t0, t0b = tmp[0]
            t1, t1b = tmp[1]
            act(lambda e, rg_t=rg_t: e.activation(t0[:], rg_t[:], AF.Exp, scale=-1.0), [rg_b], [t0b])
            act(lambda e: e.activation(t0[:], t0[:], AF.Ln, bias=1.0), [t0b], [t0b])
            act(lambda e: e.activation(t0[:], t0[:], AF.Exp, scale=-1.0), [t0b], [t0b])
            dve(lambda e, rg_t=rg_t: e.tensor_tensor(t0[:], t0[:], rg_t[:], ALU.mult), [t0b, rg_b], [t0b])
            dve(lambda e, bo=bo: e.tensor_scalar(t1[:], psum[bo][:], stats[:, 6:7], stats[:, 7:8],
                                                 ALU.subtract, ALU.mult), [psb[bo], statsb], [t1b])
            dve(lambda e: e.tensor_tensor(gated[:], t1[:], t0[:], ALU.mult), [t0b, t1b], [gatedb])
            yield
            for kt in range(2):
                bk = pbank()
                mm(bk, psum[bk][:], [(ktok[:, kt * 128:(kt + 1) * 128], v_t[:])], [ktokb, v_b])
                dve(lambda e, bk=bk, kt=kt: e.scalar_tensor_tensor(Rst[:, kt, :], Rst[:, kt, :], gch_t[:, 0:1],
                                                                    psum[bk][:], ALU.mult, ALU.add),
                    [Rstb, gch_b, psb[bk]], [Rstb])
            dve(lambda e: e.tensor_scalar(stbf[:], Rst[:], gch_t[:, 0:1], None, ALU.mult), [Rstb, gch_b], [stbfb])
            transpose_to(gated, gatedb, 4, gT, gTb, 0)
            yield
            for half in range(2):
                bk = pbank()
                mm(bk, psum[bk][:], [(gT[:, ft, :], wrob[:, ft, half * 512:(half + 1) * 512]) for ft in range(4)],
                   [gTb, wrob_b])
                dve(lambda e, bk=bk, half=half: e.tensor_copy(ys[:, half * 512:(half + 1) * 512], psum[bk][:]),
                    [psb[bk]], [ysb])
            store(SP, rsin[k][cs, 0:D], ys[:], ysb, rsinb[k])
            yield

    rsinb = [Buf("rsin%d" % k) for k in range(nsc + 1)]
    rsoutb = [Buf("rsout%d" % k) for k in range(nsc + 1)]

    def phase1_sb(k, filler=None):
        Tn = 512
        c0 = 4 * k - 3
        q_t, q_b = qs[k % 2]
        tiles = []
        for p in range(2):
            for j in range(4 * k, -1, -1):
                for h in range(2):
                    tiles.append((p, j, h, len(tiles)))
        N = len(tiles)
        filler = filler or []
        fpos = [0]
        rate = -(-len(filler) // max(1, N - 8))

        def res(t):
            p, j, h, i = t
            return dict(p=p, j=j, h=h, hs=slice(64 * h, 64 * h + 64), ob=4 + h, zb=i % 2, bb=2 + (i % 2),
                        E=Eb[i % NE], S=Sb[i % NE], X=Xb[i % 2], a=ab[i % NE], cr=crow[h],
                        first=(j == 4 * k), last=(j == 0))

        def s0(t):
            r = res(t)
            mm(r["zb"], psum[r["zb"]][:], [(kT[r["hs"], r["p"], r["j"] * 128:(r["j"] + 1) * 128], q_t[r["hs"], r["p"], :])],
               [kTb[r["p"]][r["j"]], q_b])

        def s1(t):
            r = res(t)
            zb = r["zb"]
            E_t, E_b = r["E"]
            S_t, S_b = r["S"]
            act(lambda e: e.activation(E_t[:], psum[zb][:], AF.Exp), [psb[zb]], [E_b])
            act(lambda e: e.activation(S_t[:], E_t[:], AF.Ln, bias=1.0), [E_b], [S_b])
            j = r["j"]
            mi = None
            if j >= c0:
                mi = j - c0
            elif j == 0:
                mi = 4
            if mi is not None:
                m = sbmask(mi, Tn)
                dve(lambda e: e.tensor_tensor(S_t[:], S_t[:], m, ALU.mult), [S_b, cstb], [S_b])
                dve(lambda e: e.tensor_tensor(E_t[:], E_t[:], m, ALU.mult), [E_b, cstb], [E_b])

        def s2(t):
            r = res(t)
            S_t, S_b = r["S"]
            cr_t, cr_b = r["cr"]
            terms = [(ntri, S_t[:])]
            rd = [S_b, cstb]
            if not r["first"]:
                terms.append((ones1, cr_t[:]))
                rd.append(cr_b)
            mm(r["bb"], psum[r["bb"]][:], terms, rd)

        def s3(t):
            r = res(t)
            bb = r["bb"]
            cr_t, cr_b = r["cr"]
            E_t, E_b = r["E"]
            X_t, X_b = r["X"]
            a_t, a_b = r["a"]
            if not r["last"]:
                dve(lambda e: e.tensor_copy(cr_t[:], psum[bb][0:1, :]), [psb[bb]], [cr_b])
            act(lambda e: e.activation(X_t[:], psum[bb][:], AF.Exp), [psb[bb]], [X_b])
            dve(lambda e: e.tensor_tensor(a_t[:], E_t[:], X_t[:], ALU.mult), [E_b, X_b], [a_b])

        def s4(t):
            r = res(t)
            a_t, a_b = r["a"]
            ob, hs, p, j = r["ob"], r["hs"], r["p"], r["j"]
            mm(ob, psum[ob][:], [(vsb[:, j, 128 * p:128 * p + 128], a_t[:])], [vsbb[j], a_b],
               start=r["first"], stop=r["last"])
            if r["last"]:
                dve(lambda e: e.tensor_copy(sbT[hs, p, :], psum[ob][hs, :]), [psb[ob]], [sbTb])

        for st in range(N + 4):
            if 0 <= st - 3 < N:
                s3(tiles[st - 3])
            if 0 <= st - 2 < N:
                s2(tiles[st - 2])
            if 0 <= st - 4 < N:
                s4(tiles[st - 4])
            if 0 <= st - 1 < N:
                s1(tiles[st - 1])
            if st < N:
                s0(tiles[st])
            for _ in range(rate):
                if fpos[0] < len(filler):
                    filler[fpos[0]]()
                    fpos[0] += 1
        while fpos[0] < len(filler):
            filler[fpos[0]]()
            fpos[0] += 1
        for ci in range(4):
            cs = slice(ci * 128, (ci + 1) * 128)
            for half in range(2):
                bk = pbank()
                mm(bk, psum[bk][:], [(sbT[:, p, cs], wsob[:, p, half * 512:(half + 1) * 512]) for p in range(2)],
                   [sbTb, wsob_b])
                dve(lambda e, bk=bk, half=half: e.tensor_copy(ys[:, half * 512:(half + 1) * 512], psum[bk][:]),
                    [psb[bk]], [ysb])
            store(SP, rsin[k][cs, D:2 * D], ys[:], ysb, rsinb[k])

    for _ in phase1_proj(0):
        pass
    for _ in phase1_proj(1):
        pass
    def deferred_proj(k):
        DEFER[0] = []
        for _ in phase1_proj(k):
            pass
        th = DEFER[0]
        DEFER[0] = None
        return th

    for k in range(1, nsc + 1):
        if True:
            phase1_sb(k, deferred_proj(k + 1) if k < nsc else None)
            S.dma(POOL, (lambda e, k=k: e.collective_compute("ReduceScatter", ALU.add,
                                                             replica_groups=[[0, 1, 2, 3], [4, 5, 6, 7]],
                                                             ins=[rsin[k].opt()], outs=[rsout[k].opt()])),
                  [rsinb[k]], [rsoutb[k]], rsoutb[k], inc=1)

    def barrier_all():
        evs = [(e_, S.cnt[e_]) for e_ in (PE, ACT, DVE, POOL)] + [(b_.sem, b_.inc * b_.cnt) for b_ in S.owners]
        for eng in ENGS:
            out_ = []
            for k_, v_ in evs:
                if v_ > 0 and S.seen[eng].get(k_, 0) < v_ and not (k_ == eng and eng == PE):
                    S.seen[eng][k_] = v_
                    out_.append((k_, v_))
            S.ops[eng].append((out_, None, None))

    barrier_all()
    A.off = p1_mark
    wgb, wgb_b = T("wgb", [128, 8, 2048], BF16)
    woutb, woutb_b = T("woutb", [128, 8, D], BF16)
    gp_t, gp_b = T("gpost", [128, 2 * D])
    load(SP, gp_t[:], gpost, gp_b)
    load(SP, wgb[:], wg_d.rearrange("(kt p) c -> p kt c", p=128), wgb_b, reads=[precast_bufs["wg"]])
    load(SP, woutb[:], wout_d.rearrange("(kt p) c -> p kt c", p=128), woutb_b, reads=[precast_bufs["wout"]])
    sctr = [0, 0]

    def sbank(par):
        sctr[par] += 1
        return 4 * par + (sctr[par] % 4)

    def two_streams(body, n):
        lists = []
        for t in range(n):
            DEFER[0] = []
            body(t, t % 2)
            lists.append(DEFER[0])
            DEFER[0] = None
        Aq = [th for t in range(0, n, 2) for th in lists[t]]
        Bq = [th for t in range(1, n, 2) for th in lists[t]]
        lag = len(lists[0]) // 2
        ia = ib = 0
        while ia < len(Aq) or ib < len(Bq):
            if ia < len(Aq):
                Aq[ia]()
                ia += 1
            if (ia > lag or ia >= len(Aq)) and ib < len(Bq):
                Bq[ib]()
                ib += 1

    def pctx(par, tag):
        return dict(hn=T("hn%s%d" % (tag, par), [128, D], BF16), ss=T("ss%s%d" % (tag, par), [128, 2]),
                    bank=(lambda par=par: sbank(par)))

    x2 = [T("x2_%d" % i, [128, D]) for i in range(2)]
    yrs = [T("yrs%d" % i, [128, 2 * D]) for i in range(2)]
    c2a = [pctx(i, "a") for i in range(2)]
    hnT2 = [T("hnT2_%d" % i, [128, 8, 128], BF16) for i in range(2)]
    sg_ = [T("sg%d" % i, [128, 2 * D]) for i in range(2)]
    mg_ = [T("mg%d" % i, [128, D]) for i in range(2)]
    mgh_ = [T("mgh%d" % i, [128, D], BF16) for i in range(2)]
    mgT_ = [T("mgT%d" % i, [128, 8, 128], BF16) for i in range(2)]
    h1_ = [T("h1_%d" % i, [128, D]) for i in range(2)]
    h1db = [Buf("h1d%d" % i) for i in range(NT)]

    def body2a(t, par):
        k = t + 1
        ctx = c2a[par]
        ss_, ssb_ = ctx["ss"]
        x_t, x_b = x2[par]
        y_t, y_b = yrs[par]
        hT, hTb = hnT2[par]
        sg, sgb = sg_[par]
        mg, mgb = mg_[par]
        mgh, mghb = mgh_[par]
        mgT, mgTb = mgT_[par]
        h1, h1b = h1_[par]
        load(SP, x_t[:], xown[t * 128:(t + 1) * 128, :], x_b)
        load(SP, y_t[:], rsout[k], y_b, reads=[rsoutb[k]])
        norm_and_transpose(x_t, x_b, hT, hTb, 0, ctx)
        for q4 in range(4):
            bk = sbank(par)
            mm(bk, psum[bk][:], [(hT[:, kt, :], wgb[:, kt, q4 * 512:(q4 + 1) * 512]) for kt in range(8)],
               [hTb, wgb_b])
            act(lambda e, bk=bk, q4=q4: e.activation(sg[:, q4 * 512:(q4 + 1) * 512], psum[bk][:], AF.Exp, scale=-1.0),
                [psb[bk]], [sgb])
        act(lambda e: e.activation(sg[:], sg[:], AF.Ln, bias=1.0), [sgb], [sgb])
        act(lambda e: e.activation(sg[:], sg[:], AF.Exp, scale=-1.0), [sgb], [sgb])
        dve(lambda e: e.tensor_tensor(sg[:], sg[:], y_t[:], ALU.mult), [sgb, y_b], [sgb])
        dve(lambda e: e.tensor_tensor(mgh[:], sg[:, 0:D], sg[:, D:2 * D], ALU.add), [sgb], [mghb])
        transpose_to(mgh, mghb, 8, mgT, mgTb, 0, ctx)
        for half in range(2):
            bk = sbank(par)
            mm(bk, psum[bk][:], [(mgT[:, kt, :], woutb[:, kt, half * 512:(half + 1) * 512]) for kt in range(8)],
               [mgTb, woutb_b])
            dve(lambda e, bk=bk, half=half: e.tensor_copy(mg[:, half * 512:(half + 1) * 512], psum[bk][:]),
                [psb[bk]], [mgb])
        rms_tile(mg, mgb, ctx)
        dve(lambda e: e.tensor_scalar(mg[:], mg[:], ss_[:, 1:2], None, ALU.mult), [mgb, ssb_], [mgb])
        dve(lambda e: e.tensor_tensor(mg[:], mg[:], gp_t[:, 0:D], ALU.mult), [mgb, gp_b], [mgb])
        dve(lambda e: e.tensor_tensor(h1[:], mg[:], x_t[:], ALU.add), [mgb, x_b], [h1b])
        store(SP, h1d[t * 128:(t + 1) * 128, :], h1[:], h1b, h1db[t])

    two_streams(body2a, NT)

    barrier_all()
    A.off = p1_mark
    gp2_t, gp2_b = T("gpost2", [128, D])
    load(SP, gp2_t[:], gpost[:, D:2 * D], gp2_b)
    wfib, wfib_b = T("wfib", [128, 8, 2 * DFF], BF16)
    wfob, wfob_b = T("wfob", [128, NFT, D], BF16)
    for kt_ in range(8):
        load(SP, wfib[:, kt_, :], wfi_d[kt_ * 128:(kt_ + 1) * 128, :], wfib_b, reads=[precast_bufs["wfi"]])
    for f_ in range(0, NFT, 2):
        load(SP, wfob[:, f_:f_ + 2, :], wfo_d[f_ * 128:(f_ + 2) * 128, :].rearrange("(f p) c -> p f c", p=128), wfob_b,
             reads=[precast_bufs["wfo"]])
    hb = [T("hb%d" % i, [128, D]) for i in range(2)]
    ot = [T("ot%d" % i, [128, D]) for i in range(2)]
    c2b = [pctx(i, "b") for i in range(2)]
    hnT3 = [T("hnT3_%d" % i, [128, 8, 128], BF16) for i in range(2)]
    uT_ = [T("uT%d" % i, [128, NFT, 128], BF16) for i in range(2)]
    ea_ = [T("ea%d" % i, [128, 512]) for i in range(2)]
    ff_t = [T("ff%d" % i, [128, D]) for i in range(2)]
    outb = [Buf("out%d" % i) for i in range(NT)]

    def body2b(t, par):
        ctx = c2b[par]
        ss_, ssb_ = ctx["ss"]
        h_t, h_b = hb[par]
        o_t, o_b = ot[par]
        hT, hTb = hnT3[par]
        uT, uTb = uT_[par]
        ea, eab = ea_[par]
        ff, ffb = ff_t[par]
        load(SP, h_t[:], h1d[t * 128:(t + 1) * 128, :], h_b, reads=[h1db[t]])
        norm_and_transpose(h_t, h_b, hT, hTb, 0, ctx)
        for f0 in range(0, NFT, 4):
            fn_ = min(4, NFT - f0)
            ba = sbank(par)
            bbk = sbank(par)
            for ff_ in range(fn_):
                f = f0 + ff_
                mm(ba, psum[ba][:, ff_ * 128:(ff_ + 1) * 128],
                   [(wfib[:, kt, f * 128:(f + 1) * 128], hT[:, kt, :]) for kt in range(8)], [wfib_b, hTb])
            for ff_ in range(fn_):
                f = f0 + ff_
                mm(bbk, psum[bbk][:, ff_ * 128:(ff_ + 1) * 128],
                   [(wfib[:, kt, DFF + f * 128:DFF + (f + 1) * 128], hT[:, kt, :]) for kt in range(8)],
                   [wfib_b, hTb])
            w_ = fn_ * 128
            act(lambda e, ba=ba, w_=w_: e.activation(ea[:, :w_], psum[ba][:, :w_], AF.Exp, scale=-1.0),
                [psb[ba]], [eab])
            act(lambda e, w_=w_: e.activation(ea[:, :w_], ea[:, :w_], AF.Ln, bias=1.0), [eab], [eab])
            act(lambda e, w_=w_: e.activation(ea[:, :w_], ea[:, :w_], AF.Exp, scale=-1.0), [eab], [eab])
            dve(lambda e, ba=ba, w_=w_: e.tensor_tensor(ea[:, :w_], ea[:, :w_], psum[ba][:, :w_], ALU.mult),
                [eab, psb[ba]], [eab])
            dve(lambda e, bbk=bbk, w_=w_, f0=f0, fn_=fn_: e.tensor_tensor(
                uT[:, f0:f0 + fn_, :], ea[:, :w_].rearrange("p (f t) -> p f t", t=128),
                psum[bbk][:, :w_].rearrange("p (f t) -> p f t", t=128), ALU.mult),
                [eab, psb[bbk]], [uTb])
        for half in range(2):
            bk = sbank(par)
            mm(bk, psum[bk][:], [(uT[:, f, :], wfob[:, f, half * 512:(half + 1) * 512]) for f in range(NFT)],
               [uTb, wfob_b])
            dve(lambda e, bk=bk, half=half: e.tensor_copy(ff[:, half * 512:(half + 1) * 512], psum[bk][:]),
                [psb[bk]], [ffb])
        rms_tile(ff, ffb, ctx)
        dve(lambda e: e.tensor_scalar(ff[:], ff[:], ss_[:, 1:2], None, ALU.mult), [ffb, ssb_], [ffb])
        dve(lambda e: e.tensor_tensor(ff[:], ff[:], gp2_t[:], ALU.mult), [ffb, gp2_b], [ffb])
        dve(lambda e: e.tensor_tensor(o_t[:], ff[:], h_t[:], ALU.add), [ffb, h_b], [o_b])
        store(SP, out[t * 128:(t + 1) * 128, :], o_t[:], o_b, outb[t])

    two_streams(body2b, NT)
    S.final_waits(SP, outb)

    return finish()


_NC_CACHE = {}


def _consts():
    ident = np.eye(128, dtype=np.float32)
    s = np.arange(128)
    ntri = -(s[:, None] >= s[None, :]).astype(np.float32)
    ones = np.ones((128, 128), np.float32)
    rmaskT = (s[:, None] <= s[None, :]).astype(np.float32)
    t = np.arange(512)
    masks = []
    for d in range(4):
        masks.append(((128 * d + s[:, None]) < t[None, :]).astype(np.float32))
    masks.append(np.broadcast_to((s[:, None] >= PAD), (128, 512)).astype(np.float32))
    masks.append(np.zeros((128, 512), np.float32))
    return np.concatenate([ident, ntri, ones, rmaskT] + masks, axis=1).astype(ml_dtypes.bfloat16)


def kernel(x, meta_tokens, w_in, w_ret_out, w_sb_out, w_out, w_ffn_in, w_ffn_out,
           norm_mix_pre, norm_mix_post, norm_ffn_pre, norm_ffn_post):
    x = np.asarray(x, np.float32)
    B, SEQ, _ = x.shape
    nsc = SEQ // 512
    nch = 1 + 4 * nsc
    LP = 128 * nch
    if nsc not in _NC_CACHE:
        _NC_CACHE[nsc] = build(nsc)
    nc = _NC_CACHE[nsc]
    w_in = np.asarray(w_in, np.float32)[0]
    cb = _consts()
    gam = np.concatenate([np.asarray(norm_mix_pre, np.float32)[0].reshape(8, 128).T,
                          np.asarray(norm_ffn_pre, np.float32)[0].reshape(8, 128).T], axis=1)
    gpost = np.concatenate([np.broadcast_to(np.asarray(norm_mix_post, np.float32)[0], (128, D)),
                            np.broadcast_to(np.asarray(norm_ffn_post, np.float32)[0], (128, D))], axis=1)
    pos = (np.arange(LP, dtype=np.float32) - np.float32(PAD))
    inv = (np.float32(10000.0) ** (-np.arange(128, dtype=np.float32) / np.float32(128))).astype(np.float32)
    ang = (inv[:, None] * pos[None, :]).astype(np.float32)
    cosv, sinv = np.cos(ang).astype(np.float32), np.sin(ang).astype(np.float32)
    loc = (np.arange(LP) % 128).astype(np.float64)
    in_maps = []
    for c in range(8):
        b, g = c // 4, c % 4
        log_g = np.log1p(-(2.0 ** (-5.0 - g)))
        qsc = (np.exp(log_g * (loc + 1.0)) * (256.0 ** -0.5)).astype(np.float32)
        ksc = np.exp(-log_g * (loc + 1.0)).astype(np.float32)
        rtab = np.stack([cosv * qsc, sinv * qsc, cosv * ksc, sinv * ksc]).astype(np.float32)
        gchv = np.full((128, 1), np.exp(log_g * 128.0), np.float32)
        xall = np.concatenate([np.zeros((PAD, D), np.float32), np.asarray(meta_tokens, np.float32), x[b]], axis=0)
        xown = x[b].reshape(nsc, 4, 128, D)[:, g].reshape(nsc * 128, D)
        cols = np.concatenate([
            np.arange(256 * g, 256 * g + 256),
            1024 + np.arange(256 * g, 256 * g + 256),
            2048 + np.arange(512 * g, 512 * g + 512),
            4096 + np.arange(512 * g, 512 * g + 512),
            6144 + np.arange(256 * g, 256 * g + 256),
            7168 + np.arange(256 * g, 256 * g + 256),
            8192 + np.arange(256 * g, 256 * g + 256),
        ])
        in_maps.append({
            "xall": np.ascontiguousarray(xall),
            "xown": np.ascontiguousarray(xown),
            "w1": np.ascontiguousarray(w_in[:, cols]),
            "wg": np.ascontiguousarray(w_in[:, 9216:11264]),
            "wro": np.ascontiguousarray(np.asarray(w_ret_out, np.float32)[0][512 * g:512 * g + 512]),
            "wso": np.ascontiguousarray(np.asarray(w_sb_out, np.float32)[0][256 * g:256 * g + 256]),
            "wout": np.ascontiguousarray(np.asarray(w_out, np.float32)[0]),
            "wfi": np.ascontiguousarray(np.asarray(w_ffn_in, np.float32)[0]),
            "wfo": np.ascontiguousarray(np.asarray(w_ffn_out, np.float32)[0]),
            "gam": np.ascontiguousarray(gam),
            "gpost": np.ascontiguousarray(gpost),
            "rtab": np.ascontiguousarray(rtab),
            "gch": gchv,
            "cbf": cb,
        })
    res = run_bass_kernel_spmd(nc, in_maps, core_ids=list(range(8)))
    outp = np.zeros((B, SEQ, D), np.float32)
    for c in range(8):
        b, g = c // 4, c % 4
        o = np.asarray(res.results[c]["out"], np.float32).reshape(nsc, 128, D)
        outp[b].reshape(nsc, 4, 128, D)[:, g] = o
    return outp
```
